# Optimizing a Trainium2 kernel written in Bass

```python
import math
import jax, jax.numpy as jnp
from jax import lax
import numpy as np

D_MODEL = 1024
BATCH = 8
SEQ = 8192
DEPTH = 2

EPS = 1e-6
MIX_WIDTH = D_MODEL // 2
HGRN_HEADS = 4
HGRN_DK = MIX_WIDTH // HGRN_HEADS
HGRN_DV = MIX_WIDTH // HGRN_HEADS
HGRN_CHUNK = 64
DIFF_HEADS = 4
DIFF_DQK = MIX_WIDTH // (2 * DIFF_HEADS)
DIFF_DV = MIX_WIDTH // DIFF_HEADS
Q_BLOCK = 128
GMLP_GROUPS = 4
GMLP_DG = MIX_WIDTH // GMLP_GROUPS
GMLP_CHUNK = 128
D_FF = 4 * D_MODEL
N_BRANCH = 3
IN_SIZES = [MIX_WIDTH] * 4 + [MIX_WIDTH] * 3 + [MIX_WIDTH] * 2 + [N_BRANCH * D_MODEL]
IN_COLS = sum(IN_SIZES)

kernel_name = "hybrid_hgrn2_diffattn_gmlp_gated"


def rmsnorm(x, w):
    xf = x.astype(jnp.float32)
    y = xf * lax.rsqrt(jnp.mean(xf * xf, axis=-1, keepdims=True) + EPS)
    return (y * w.astype(jnp.float32)).astype(x.dtype)


def layernorm(x, w, b):
    xf = x.astype(jnp.float32)
    mu = jnp.mean(xf, axis=-1, keepdims=True)
    var = jnp.mean(jnp.square(xf - mu), axis=-1, keepdims=True)
    y = (xf - mu) * lax.rsqrt(var + EPS)
    return (y * w.astype(jnp.float32) + b.astype(jnp.float32)).astype(x.dtype)


def hgrn2_mixer(q, f_logit, i, g, lb, norm_w):
    B, S, _ = q.shape
    dt = q.dtype
    f32 = jnp.float32
    C = HGRN_CHUNK
    nc = S // C
    qf = jax.nn.silu(q.astype(f32))
    lbf = lb.astype(f32)
    log_f = jnp.logaddexp(jnp.log(lbf), jnp.log1p(-lbf) + jax.nn.log_sigmoid(f_logit.astype(f32)))
    kf = -jnp.expm1(log_f)
    vf = i.astype(f32)

    def to_chunks(t, d):
        return t.reshape(B, nc, C, HGRN_HEADS, d).transpose(1, 0, 3, 2, 4)

    qc, kc, lfc = to_chunks(qf, HGRN_DK), to_chunks(kf, HGRN_DK), to_chunks(log_f, HGRN_DK)
    vc = to_chunks(vf, HGRN_DV)
    causal = jnp.tril(jnp.ones((C, C), dtype=bool))

    def step(state, inp):
        qb, kb, lfb, vb = inp
        b = jnp.cumsum(lfb, axis=-2)
        o_inter = jnp.einsum('bhtk,bhkv->bhtv', qb * jnp.exp(b), state)
        rel = b[..., :, None, :] - b[..., None, :, :]
        decay = jnp.exp(jnp.where(causal[:, :, None], rel, -jnp.inf))
        scores = jnp.einsum('bhtk,bhsk,bhtsk->bhts', qb, kb, decay)
        o = o_inter + jnp.einsum('bhts,bhsv->bhtv', scores, vb)
        b_last = b[..., -1:, :]
        k_dec = kb * jnp.exp(b_last - b)
        state = jnp.exp(b_last[..., 0, :])[..., None] * state + jnp.einsum('bhsk,bhsv->bhkv', k_dec, vb)
        return state, o

    state0 = jnp.zeros((B, HGRN_HEADS, HGRN_DK, HGRN_DV), f32)
    _, oc = lax.scan(step, state0, (qc, kc, lfc, vc))
    o = oc.transpose(1, 0, 3, 2, 4).reshape(B, S, HGRN_HEADS, HGRN_DV)
    o = rmsnorm(o, norm_w) * jax.nn.silu(g.astype(f32).reshape(B, S, HGRN_HEADS, HGRN_DV))
    return o.reshape(B, S, MIX_WIDTH).astype(dt)


def diff_attention(q, k, v, lam_q1, lam_k1, lam_q2, lam_k2, lam_init, norm_w):
    B, S, _ = q.shape
    dt = q.dtype
    f32 = jnp.float32
    H, d = DIFF_HEADS, DIFF_DQK
    qf = q.astype(f32).reshape(B, S, H, 2, d)
    kf = k.astype(f32).reshape(B, S, H, 2, d)
    vf = v.astype(f32).reshape(B, S, H, DIFF_DV)
    lam = (jnp.exp(jnp.sum(lam_q1.astype(f32) * lam_k1.astype(f32)))
           - jnp.exp(jnp.sum(lam_q2.astype(f32) * lam_k2.astype(f32))) + lam_init)
    scale = d ** -0.5
    nb = S // Q_BLOCK
    qb = qf.reshape(B, nb, Q_BLOCK, H, 2, d).transpose(1, 0, 2, 3, 4, 5)
    kpos = jnp.arange(S)

    def block(args):
        qblk, blk = args
        s = jnp.einsum('bqhcd,bkhcd->bhcqk', qblk, kf) * scale
        qpos = blk * Q_BLOCK + jnp.arange(Q_BLOCK)
        s = jnp.where(kpos[None, :] <= qpos[:, None], s, -jnp.inf)
        p = jax.nn.softmax(s, axis=-1)
        a = p[:, :, 0] - lam * p[:, :, 1]
        return jnp.einsum('bhqk,bkhv->bqhv', a, vf)

    o = lax.map(block, (qb, jnp.arange(nb)))
    o = o.transpose(1, 0, 2, 3, 4).reshape(B, S, H, DIFF_DV)
    o = rmsnorm(o, norm_w) * (1.0 - lam_init)
    return o.reshape(B, S, MIX_WIDTH).astype(dt)


def chunked_gmlp(u, v, ln_w, ln_b, w_s, b_s):
    B, S, _ = u.shape
    C = GMLP_CHUNK
    nc = S // C
    u = jax.nn.gelu(u, approximate=False)
    v = layernorm(jax.nn.gelu(v, approximate=False), ln_w, ln_b)
    vc = v.reshape(B, nc, C, GMLP_GROUPS, GMLP_DG)
    w = w_s * jnp.tril(jnp.ones((C, C), dtype=w_s.dtype))[None]
    mixed = jnp.einsum('gts,bnsgd->bntgd', w, vc) + b_s.T[:, :, None]
    return u * mixed.reshape(B, S, MIX_WIDTH)


def setup_inputs(seed: int = 0) -> dict:
    key = jax.random.key(seed)
    ks = jax.random.split(key, 24)
    f32 = jnp.float32
    nrm = lambda k, shape, s: jax.random.normal(k, shape, f32) * s
    gain = lambda k, shape: 1.0 + nrm(k, shape, 0.02)
    return {
        "x": nrm(ks[0], (BATCH, SEQ, D_MODEL), 1.0),
        "norm_mix_w": gain(ks[1], (DEPTH, D_MODEL)),
        "w_in": nrm(ks[2], (DEPTH, D_MODEL, IN_COLS), D_MODEL ** -0.5),
        "hgrn_lb_logits": nrm(ks[3], (DEPTH, MIX_WIDTH), 0.5),
        "hgrn_norm_w": gain(ks[4], (DEPTH, HGRN_DV)),
        "diff_lam_q1": nrm(ks[5], (DEPTH, DIFF_DQK), 0.1),
        "diff_lam_k1": nrm(ks[6], (DEPTH, DIFF_DQK), 0.1),
        "diff_lam_q2": nrm(ks[7], (DEPTH, DIFF_DQK), 0.1),
        "diff_lam_k2": nrm(ks[8], (DEPTH, DIFF_DQK), 0.1),
        "diff_norm_w": gain(ks[9], (DEPTH, DIFF_DV)),
        "gmlp_ln_w": gain(ks[10], (DEPTH, MIX_WIDTH)),
        "gmlp_ln_b": nrm(ks[11], (DEPTH, MIX_WIDTH), 0.02),
        "gmlp_w_s": nrm(ks[12], (DEPTH, GMLP_GROUPS, GMLP_CHUNK, GMLP_CHUNK), GMLP_CHUNK ** -0.5),
        "gmlp_b_s": 1.0 + nrm(ks[13], (DEPTH, GMLP_GROUPS, GMLP_CHUNK), 0.1),
        "w_br_hgrn": nrm(ks[14], (DEPTH, MIX_WIDTH, D_MODEL), MIX_WIDTH ** -0.5),
        "w_br_attn": nrm(ks[15], (DEPTH, MIX_WIDTH, D_MODEL), MIX_WIDTH ** -0.5),
        "w_br_gmlp": nrm(ks[16], (DEPTH, MIX_WIDTH, D_MODEL), MIX_WIDTH ** -0.5),
        "w_out": nrm(ks[17], (DEPTH, D_MODEL, D_MODEL), D_MODEL ** -0.5),
        "norm_ff_w": gain(ks[18], (DEPTH, D_MODEL)),
        "w_ff1": nrm(ks[19], (DEPTH, D_MODEL, D_FF), D_MODEL ** -0.5),
        "w_ff2": nrm(ks[20], (DEPTH, D_FF, D_MODEL), D_FF ** -0.5),
        "final_norm_w": gain(ks[21], (D_MODEL,)),
    }


def reference(x, norm_mix_w, w_in, hgrn_lb_logits, hgrn_norm_w, diff_lam_q1, diff_lam_k1,
              diff_lam_q2, diff_lam_k2, diff_norm_w, gmlp_ln_w, gmlp_ln_b, gmlp_w_s, gmlp_b_s,
              w_br_hgrn, w_br_attn, w_br_gmlp, w_out, norm_ff_w, w_ff1, w_ff2, final_norm_w):
    B, S, _ = x.shape
    p = jax.nn.softmax(hgrn_lb_logits.astype(jnp.float32), axis=0)
    cum = jnp.cumsum(p, axis=0)
    lbs = cum - cum[0:1]
    split_idx = [int(v) for v in np.cumsum(IN_SIZES)[:-1]]

    for l in range(DEPTH):
        h = rmsnorm(x, norm_mix_w[l])
        z = jnp.einsum('bsd,dc->bsc', h, w_in[l])
        hq, hf, hi, hg, aq, ak, av, gu, gv, gate_logits = jnp.split(z, split_idx, axis=-1)

        y_h = hgrn2_mixer(hq, hf, hi, hg, lbs[l], hgrn_norm_w[l])
        lam_init = 0.8 - 0.6 * math.exp(-0.3 * l)
        y_a = diff_attention(aq, ak, av, diff_lam_q1[l], diff_lam_k1[l], diff_lam_q2[l],
                             diff_lam_k2[l], lam_init, diff_norm_w[l])
        y_g = chunked_gmlp(gu, gv, gmlp_ln_w[l], gmlp_ln_b[l], gmlp_w_s[l], gmlp_b_s[l])

        gates = jax.nn.sigmoid(gate_logits.astype(jnp.float32)).astype(x.dtype)
        gates = gates.reshape(B, S, N_BRANCH, D_MODEL)
        merged = (gates[:, :, 0] * jnp.einsum('bsm,md->bsd', y_h, w_br_hgrn[l])
                  + gates[:, :, 1] * jnp.einsum('bsm,md->bsd', y_a, w_br_attn[l])
                  + gates[:, :, 2] * jnp.einsum('bsm,md->bsd', y_g, w_br_gmlp[l]))
        x = x + jnp.einsum('bsd,de->bse', merged, w_out[l])

        h2 = rmsnorm(x, norm_ff_w[l])
        ff = jnp.square(jax.nn.relu(jnp.einsum('bsd,df->bsf', h2, w_ff1[l])))
        x = x + jnp.einsum('bsf,fd->bsd', ff, w_ff2[l])

    return rmsnorm(x, final_norm_w)
```

```python
import math
from contextlib import ExitStack

import numpy as np
import concourse.bass as bass
import concourse.mybir as mybir
from concourse.bass_utils import run_bass_kernel_spmd

F32 = mybir.dt.float32
BF16 = mybir.dt.bfloat16
U32 = mybir.dt.uint32
AF = mybir.ActivationFunctionType
ALU = mybir.AluOpType
AX = mybir.AxisListType

D = 1024
MW = 512
INC = 7680
DFF = 4096
EPS = 1e-6
T = 512
SEM_LIMIT = 30000


class Tok:
    __slots__ = ("sem", "val", "eng")

    def __init__(self, sem, val, eng):
        self.sem, self.val, self.eng = sem, val, eng


class Buf:
    def __init__(self, name):
        self.name = name
        self.w = None
        self.r = {}
        self.d = {}


class TB(Buf):
    def __init__(self, name, t):
        super().__init__(name)
        self.t = t

    def __getitem__(self, k):
        return self.t[k]


class Sched:
    ENGS = ("pe", "act", "dve", "pool", "sp")

    def __init__(self, nc, stack):
        self.nc, self.stack = nc, stack
        self.ops = {e: [] for e in self.ENGS}
        self.esem = {}
        self.ecnt = {}
        self.known = {e: {} for e in self.ENGS}
        self.nsem = 0
        for e in self.ENGS:
            self._new_esem(e)
        self.dma_bufs = []

    def _sem(self, name):
        self.nsem += 1
        return self.stack.enter_context(self.nc.semaphore(f"{name}_{self.nsem}"))

    def _new_esem(self, e):
        self.esem[e] = self._sem("e" + e)
        self.ecnt[e] = 0

    def _wait(self, eng, tok):
        if tok is None:
            return
        if tok.eng == "pe" and eng == "pe":
            return
        k = self.known[eng]
        key = id(tok.sem)
        if k.get(key, 0) >= tok.val:
            return
        k[key] = tok.val
        self.ops[eng].append(("w", tok.sem, tok.val))

    def _deps(self, eng, reads, writes):
        for b in reads:
            self._wait(eng, b.w)
        for b in writes:
            self._wait(eng, b.w)
            for t in b.r.values():
                self._wait(eng, t)

    def _mark(self, tok, reads, writes):
        for b in reads:
            b.r[id(tok.sem)] = tok
        for b in writes:
            b.w = tok
            b.r = {}

    def op(self, eng, fn, reads=(), writes=()):
        self._deps(eng, reads, writes)
        if self.ecnt[eng] >= SEM_LIMIT:
            self._new_esem(eng)
        self.ecnt[eng] += 1
        tok = Tok(self.esem[eng], self.ecnt[eng], eng)
        self.ops[eng].append(("o", fn, tok.sem, 1))
        self._mark(tok, reads, writes)
        return tok

    def dma(self, q, out_ap, in_ap, reads, writes, owner, serialize=True, **kw):
        self._deps(q, reads, writes)
        kind = "sw" if q == "pool" else "hw"
        st = owner.d.get(kind)
        if st is None:
            st = owner.d[kind] = {"sem": None, "cnt": 0, "last": None}
            self.dma_bufs.append(st)
        if st["sem"] is None or st["cnt"] >= SEM_LIMIT:
            if st["last"] is not None:
                self._wait(q, st["last"])
            st["sem"] = self._sem("d")
            st["cnt"] = 0
        if serialize and st["last"] is not None:
            self._wait(q, st["last"])
        st["cnt"] += 16
        tok = Tok(st["sem"], st["cnt"], "dma")
        st["last"] = tok

        def fn(e, out_ap=out_ap, in_ap=in_ap, kw=kw):
            return e.dma_start(out=out_ap, in_=in_ap, **kw)

        self.ops[q].append(("o", fn, tok.sem, 16))
        self._mark(tok, reads, writes)
        return tok

    def fence(self):
        toks = [Tok(self.esem[e], self.ecnt[e], e) for e in self.ENGS if self.ecnt[e] > 0]
        dt = [st["last"] for st in self.dma_bufs if st["last"] is not None]
        for e in self.ENGS:
            for t in toks:
                if t.eng != e:
                    self._wait(e, t)
            for t in dt:
                self._wait(e, t)

    def finish(self, q="sp"):
        for st in self.dma_bufs:
            if st["last"] is not None:
                self._wait(q, st["last"])

    def emit(self, block):
        nc = self.nc

        def run(e, lst):
            for it in lst:
                if it[0] == "w":
                    e.wait_ge(it[1], it[2])
                else:
                    ins = it[1](e)
                    ins.then_inc(it[2], it[3])

        @block.tensor
        def _(e):
            run(e, self.ops["pe"])

        @block.scalar
        def _(e):
            run(e, self.ops["act"])

        @block.vector
        def _(e):
            run(e, self.ops["dve"])

        @block.gpsimd
        def _(e):
            run(e, self.ops["pool"])

        @block.sync
        def _(e):
            run(e, self.ops["sp"])


class Builder:
    def __init__(self, S, n_layers=2, debug_out=None):
        self.S = S
        self.NT = S // T
        self.L = n_layers
        self.debug_out = debug_out
        self.nc = bass.Bass("TRN2", target_bir_lowering=False)
        self.sbuf_bytes = 0

    def sb(self, name, shape, dtype):
        self.sb_n = getattr(self, "sb_n", 0) + 1
        name = f"{name}_{self.sb_n}"
        t = self.stack.enter_context(self.nc.sbuf_tensor(name, list(shape), dtype))
        n = 1
        for s in shape[1:]:
            n *= s
        self.sbuf_bytes += n * (2 if dtype == BF16 else 4)
        return TB(name, t)

    def dram(self, name, shape, dtype, kind="Internal"):
        return self.nc.dram_tensor(name, list(shape), dtype, kind=kind)

    def mm(self, group, reads, writes):
        def fn(e, group=group):
            ins = None
            for g in group:
                o, l, r, st, sp = g[:5]
                if len(g) > 5:
                    ins = e.matmul(o, l, r, start=st, stop=sp, tile_position=g[5])
                else:
                    ins = e.matmul(o, l, r, start=st, stop=sp)
            return ins
        return self.s.op("pe", fn, reads, writes)

    def tr(self, group, reads, writes):
        def fn(e, group=group):
            ins = None
            for (o, i, idn) in group:
                ins = e.transpose(o, i, idn)
            return ins
        return self.s.op("pe", fn, reads, writes)

    def act(self, out, in_, func, reads, writes, **kw):
        def fn(e):
            return e.activation(out, in_, func, **kw)
        return self.s.op("act", fn, reads, writes)

    def tt(self, eng, out, in0, in1, op, reads, writes):
        def fn(e):
            return e.tensor_tensor(out, in0, in1, op)
        return self.s.op(eng, fn, reads, writes)

    def ts(self, eng, out, in0, s1, s2, op0, op1, reads, writes):
        def fn(e):
            if s2 is None:
                return e.tensor_scalar(out, in0, s1, None, op0)
            return e.tensor_scalar(out, in0, s1, s2, op0, op1)
        return self.s.op(eng, fn, reads, writes)

    def stt(self, eng, out, in0, scalar, in1, op0, op1, reads, writes):
        def fn(e):
            return e.scalar_tensor_tensor(out, in0, scalar, in1, op0, op1)
        return self.s.op(eng, fn, reads, writes)

    def rsqrt(self, out, in_, reads, writes, scale=1.0):
        self.act(out, in_, AF.Sqrt, reads, writes, bias=self.eps_col[:, 0:1], scale=scale)
        self.s.op("dve", lambda e: e.reciprocal(out, out), writes, writes)

    def rsqrt_el(self, out, in_, reads, writes):
        self.act(out, in_, AF.Ln, reads, writes, bias=self.eps_col[:, 0:1], scale=1.0)
        self.act(out, out, AF.Exp, writes, writes, scale=-0.5)

    def cp(self, eng, out, in_, reads, writes):
        def fn(e):
            return e.tensor_copy(out, in_)
        return self.s.op(eng, fn, reads, writes)

    def bank(self):
        b = self.banks[self.bank_i % len(self.banks)]
        self.bank_i += 1
        return b

    def build(self):
        nc = self.nc
        S, L = self.S, self.L
        with ExitStack() as stack:
            self.stack = stack
            self.s = Sched(nc, stack)
            self.declare_io()
            self.alloc_common()
            self.constants()
            self.convert_weights()
            self.alloc_layer_consts()
            self.flush()
            outer = stack
            for l in range(L):
                for ph in (self.phase_a, self.phase_b1, self.phase_c):
                    with ExitStack() as ps:
                        self.stack = ps
                        ph(l)
                        self.flush()
                    self.stack = outer
            self.s.finish("sp")
            self.flush()
        return nc

    def flush(self):
        self.s.fence()
        with self.nc.Block() as block:
            self.s.emit(block)
        self.s.ops = {e: [] for e in Sched.ENGS}

    def declare_io(self):
        S, L = self.S, self.L
        ein = lambda n, sh: self.nc.dram_tensor(n, list(sh), F32, kind="ExternalInput")
        self.x_in = ein("x", (S, D))
        self.i = {}
        for n, sh in [("norm_mix_w", (2, D)), ("w_in", (2, D, INC)), ("hgrn_lb_logits", (2, MW)),
                      ("hgrn_norm_w", (2, 128)), ("diff_lam_q1", (2, 64)), ("diff_lam_k1", (2, 64)),
                      ("diff_lam_q2", (2, 64)), ("diff_lam_k2", (2, 64)), ("diff_norm_w", (2, 128)),
                      ("gmlp_ln_w", (2, MW)), ("gmlp_ln_b", (2, MW)), ("gmlp_w_s", (2, 4, 128, 128)),
                      ("gmlp_b_s", (2, 4, 128)), ("w_br_hgrn", (2, MW, D)), ("w_br_attn", (2, MW, D)),
                      ("w_br_gmlp", (2, MW, D)), ("w_out", (2, D, D)), ("norm_ff_w", (2, D)),
                      ("w_ff1", (2, D, DFF)), ("w_ff2", (2, DFF, D)), ("final_norm_w", (1, D))]:
            self.i[n] = ein(n, sh)
        self.y_out = self.nc.dram_tensor("y", [S, D], F32, kind="ExternalOutput")
        NT = self.NT
        self.xs = self.dram("xs", (S, D), F32)
        self.xs_b = [Buf(f"xs{i}") for i in range(NT)]
        self.x_b = [Buf(f"xin{i}") for i in range(NT)]
        self.y_b = [Buf(f"y{i}") for i in range(NT)]
        self.qT = self.dram("qT", (4, 128, S), BF16)
        self.kT = self.dram("kT", (4, 128, S), BF16)
        self.vA = self.dram("vA", (4, 128, S // 128, 128), BF16)
        self.gA = self.dram("gA", (8, 128, S), BF16)
        self.pm = self.dram("pm", (8, 128, S), F32)
        self.yA = self.dram("yA", (4, 128, S), BF16)
        self.mg = self.dram("mg", (8, 128, S), BF16)
        self.mg_b = [Buf(f"mg{i}") for i in range(NT)]
        mk = lambda n: [Buf(f"{n}{i}") for i in range(NT)]
        self.qT_b, self.kT_b, self.vA_b, self.gA_b, self.pm_b, self.yA_b = (
            mk("qT"), mk("kT"), mk("vA"), mk("gA"), mk("pm"), mk("yA"))
        self.wb = {}
        self.wb_b = {}
        for l in range(L):
            for n, nu in [("w_in", 15), ("w_br_hgrn", 1), ("w_br_attn", 1), ("w_br_gmlp", 1),
                          ("w_out", 2), ("w_ff1", 8), ("w_ff2", 8)]:
                self.wb[(n, l)] = self.dram(f"wb_{n}_{l}", (nu, 128, 4096), BF16)
                self.wb_b[(n, l)] = Buf(f"wb_{n}_{l}")

    def alloc_common(self):
        nc = self.nc
        self.banks = []
        for i in range(8):
            t = self.stack.enter_context(nc.psum_tensor(f"bank{i}", [128, 512], F32))
            self.banks.append(TB(f"bank{i}", t))
        self.bank_i = 0
        sb = self.sb
        self.ring = [sb(f"ring{i}", (128, 4096), BF16) for i in range(3)]
        self.ring_i = 0
        self.xt = sb("xt", (128, 4, D), F32)
        self.hb = sb("hb", (128, 4, D), BF16)
        self.hT = sb("hT", (128, 8, T), BF16)
        self.tmp = [sb(f"tmp{i}", (128, T), F32) for i in range(4)]
        self.tmp_i = 0
        self.small = sb("small", (128, 64), F32)
        self.small2 = sb("small2", (128, 64), F32)

    def tmpb(self):
        t = self.tmp[self.tmp_i % 4]
        self.tmp_i += 1
        return t

    def run_conv_jobs(self, n, max_layer=None):
        for _ in range(n):
            if self.conv_jobs and (max_layer is None or self.conv_jobs[0][0] <= max_layer):
                self.conv_jobs.pop(0)[1]()

    def load_unit(self, name, l, u):
        r = self.ring[self.ring_i % 3]
        self.ring_i += 1
        self.s.dma("sp", r[:, :], self.wb[(name, l)].ap()[u], [self.wb_b[(name, l)]], [r], r)
        return r

    def bcast_row(self, dst, src_row_ap, n):
        self.s.dma("sp", dst[:, 0:n], src_row_ap.partition_broadcast(128), [], [dst], dst)

    def constants(self):
        nc, s, sb = self.nc, self.s, self.sb
        self.J = sb("iotaJ", (128, 128), F32)
        self.P = sb("iotaP", (128, 128), F32)
        s.op("pool", lambda e: e.iota(self.J[:, :], [[1, 128]], base=0, channel_multiplier=0,
                                      allow_small_or_imprecise_dtypes=True), [], [self.J])
        s.op("pool", lambda e: e.iota(self.P[:, :], [[0, 128]], base=0, channel_multiplier=1,
                                      allow_small_or_imprecise_dtypes=True), [], [self.P])
        J, P = self.J, self.P
        self.ident = sb("ident", (128, 128), BF16)
        self.tt("dve", self.ident[:, :], J[:, :], P[:, :], ALU.is_equal, [J, P], [self.ident])
        self.amask = sb("amask", (128, 128), BF16)
        self.tt("dve", self.amask[:, :], P[:, :], J[:, :], ALU.is_le, [J, P], [self.amask])
        self.ones_m = sb("ones_m", (128, 128), F32)
        s.op("dve", lambda e: e.memset(self.ones_m[:, :], 1.0 / 128.0), [], [self.ones_m])
        self.ones_b = sb("ones_b", (128, 128), BF16)
        s.op("dve", lambda e: e.memset(self.ones_b[:, :], 1.0), [], [self.ones_b])
        self.sel1 = sb("sel1", (128, 128), F32)
        self.sel2 = sb("sel2", (128, 128), F32)
        t0 = self.tmpb()
        self.ts("dve", self.sel1[:, :], P[:, :], 31.0, 1.0 / 32.0, ALU.is_le, ALU.mult, [P], [self.sel1])
        self.ts("dve", t0[:, 0:128], P[:, :], 64.0, 1.0 / 32.0, ALU.is_ge, ALU.mult, [P], [t0])
        self.ts("dve", t0[:, 128:256], P[:, :], 95.0, None, ALU.is_le, None, [P], [t0])
        self.tt("dve", t0[:, 0:128], t0[:, 0:128], t0[:, 128:256], ALU.mult, [t0], [t0])
        self.tt("dve", self.sel1[:, :], self.sel1[:, :], t0[:, 0:128], ALU.add, [self.sel1, t0], [self.sel1])
        self.ts("dve", self.sel2[:, :], self.sel1[:, :], -1.0, 1.0 / 32.0, ALU.mult, ALU.add, [self.sel1], [self.sel2])
        self.eps_col = sb("eps_col", (128, 1), F32)
        s.op("dve", lambda e: e.memset(self.eps_col[:, :], EPS), [], [self.eps_col])
        self.ones_f = sb("ones_f", (128, 128), F32)
        s.op("dve", lambda e: e.memset(self.ones_f[:, :], 1.0), [], [self.ones_f])

    def constants_a(self):
        nc, s, sb = self.nc, self.s, self.sb
        J, P = self.J, self.P
        self.identf = sb("identf", (128, 128), F32)
        self.tt("dve", self.identf[:, :], J[:, :], P[:, :], ALU.is_equal, [J, P], [self.identf])
        self.TriP = sb("TriP", (128, 128), F32)
        t0 = self.tmpb()
        self.tt("dve", self.TriP[:, :], P[:, :], J[:, :], ALU.is_le, [J, P], [self.TriP])
        self.ts("dve", t0[:, 0:128], P[:, :], 63.0, None, ALU.is_le, None, [P], [t0])
        self.tt("dve", self.TriP[:, :], self.TriP[:, :], t0[:, 0:128], ALU.subtract, [self.TriP, t0], [self.TriP])
        self.TriPP = sb("TriPP", (128, 128), F32)
        self.tt("dve", self.TriPP[:, :], P[:, :], J[:, :], ALU.is_gt, [J, P], [self.TriPP])
        self.TriX = sb("TriX", (128, 2), F32)
        self.ts("dve", self.TriX[:, 0:1], P[:, 0:1], 63.0, None, ALU.is_le, None, [P], [self.TriX])
        self.ts("dve", self.TriX[:, 1:2], P[:, 0:1], 0.0, None, ALU.is_ge, None, [P], [self.TriX])
        self.hmask = sb("hmask", (128, 4, 128), U32)
        for h in range(4):
            self.tt("dve", self.hmask[:, h, :], P[:, :], J[:, :], ALU.is_le, [J, P], [self.hmask])
        self.gmask = sb("gmask", (128, 128), F32)
        self.tt("dve", self.gmask[:, :], J[:, :], P[:, :], ALU.is_le, [J, P], [self.gmask])

    def convert_weights(self):
        s = self.s
        self.conv_jobs = []
        for l in range(self.L):
            def conv(name, src_units, l=l):
                owner = self.wb_b[(name, l)]
                dst = self.wb[(name, l)].ap()
                for u, src in enumerate(src_units):
                    def job(u=u, src=src, owner=owner, dst=dst):
                        s.dma("pool", dst[u].rearrange("p (c j) -> p c j", c=src.shape[1]), src, [], [owner],
                              owner, serialize=False)
                    if l == 0 and name in ("w_in", "w_br_hgrn", "w_br_gmlp"):
                        job()
                    else:
                        self.conv_jobs.append((l, job))
            w = self.i["w_in"].ap()[l]
            wv = w.rearrange("(c p) (u j) -> u p c j", p=128, j=512)
            conv("w_in", [wv[u] for u in range(15)])
            for n in ("w_br_hgrn", "w_br_attn", "w_br_gmlp"):
                w = self.i[n].ap()[l]
                conv(n, [w.rearrange("(c p) j -> p c j", p=128)])
            w = self.i["w_out"].ap()[l]
            wv = w.rearrange("(c p) (u j) -> u p c j", p=128, j=512)
            conv("w_out", [wv[u] for u in range(2)])
            w = self.i["w_ff1"].ap()[l]
            wv = w.rearrange("(c p) (u j) -> u p c j", p=128, j=512)
            conv("w_ff1", [wv[u] for u in range(8)])
            w = self.i["w_ff2"].ap()[l]
            wv = w.rearrange("(u c p) j -> u p c j", p=128, c=4)
            conv("w_ff2", [wv[u] for u in range(8)])

    def layer_consts(self, l):
        s, sb, i = self.s, self.sb, self.i
        lg = i["hgrn_lb_logits"].ap()
        if l == 0:
            s.op("dve", lambda e: e.memset(self.lb_rep[:, :], 0.0), [], [self.lb_rep])
            s.op("dve", lambda e: e.memset(self.lb_col[:, :], 0.0), [], [self.lb_col])
        else:
            t0, t1 = self.tmpb(), self.tmpb()
            self.bcast_row(t0, lg[0:1, :], MW)
            self.bcast_row(t1, lg[1:2, :], MW)
            self.tt("dve", t1[:, :], t1[:, :], t0[:, :], ALU.subtract, [t0, t1], [t1])
            self.act(self.lb_rep[:, :], t1[:, :], AF.Sigmoid, [t1], [self.lb_rep])
            sm = self.small
            s.dma("sp", sm[:, 0:4], lg[0].rearrange("(h p) -> p h", p=128), [], [sm], sm,
                  allow_slow_non_contiguous=True)
            s.dma("sp", sm[:, 4:8], lg[1].rearrange("(h p) -> p h", p=128), [], [sm], sm,
                  allow_slow_non_contiguous=True)
            self.tt("dve", sm[:, 8:12], sm[:, 4:8], sm[:, 0:4], ALU.subtract, [sm], [sm])
            self.act(self.lb_col[:, :], sm[:, 8:12], AF.Sigmoid, [sm], [self.lb_col])
        self.ts("dve", self.oml_rep[:, :], self.lb_rep[:, :], -1.0, 1.0, ALU.mult, ALU.add, [self.lb_rep], [self.oml_rep])
        self.ts("dve", self.oml_col[:, :], self.lb_col[:, :], -1.0, 1.0, ALU.mult, ALU.add, [self.lb_col], [self.oml_col])
        self.ts("dve", self.lbm1_col[:, :], self.lb_col[:, :], -1.0, None, ALU.add, None, [self.lb_col], [self.lbm1_col])
        s.dma("sp", self.hnw[:, 0:1], i["hgrn_norm_w"].ap()[l].rearrange("(p o) -> p o", o=1), [], [self.hnw], self.hnw,
              allow_slow_non_contiguous=True)
        s.dma("sp", self.dnw[:, 0:1], i["diff_norm_w"].ap()[l].rearrange("(p o) -> p o", o=1), [], [self.dnw], self.dnw,
              allow_slow_non_contiguous=True)
        lam_init = 0.8 - 0.6 * math.exp(-0.3 * l)
        self.ts("dve", self.dnw[:, :], self.dnw[:, :], 1.0 - lam_init, None, ALU.mult, None, [self.dnw], [self.dnw])
        s.dma("sp", self.lnw_col[:, :], i["gmlp_ln_w"].ap()[l].rearrange("(g p) -> p g", p=128), [], [self.lnw_col],
              self.lnw_col, allow_slow_non_contiguous=True)
        lv = self.lamv
        for j, n in enumerate(("diff_lam_q1", "diff_lam_k1", "diff_lam_q2", "diff_lam_k2")):
            s.dma("sp", lv[:, j:j + 1], i[n].ap()[l].rearrange("(p o) -> p o", o=1), [], [lv], lv,
                  allow_slow_non_contiguous=True)
        self.tt("dve", lv[:, 4:5], lv[:, 0:1], lv[:, 1:2], ALU.mult, [lv], [lv])
        self.tt("dve", lv[:, 5:6], lv[:, 2:3], lv[:, 3:4], ALU.mult, [lv], [lv])
        pb = self.bank()
        self.mm([(pb[:, 0:2], self.ones_f[0:64, :], lv[:, 4:6], True, True)], [self.ones_f, lv], [pb])
        self.act(self.lam[:, 2:4], pb[:, 0:2], AF.Exp, [pb], [self.lam])
        self.tt("dve", self.lam[:, 0:1], self.lam[:, 2:3], self.lam[:, 3:4], ALU.subtract, [self.lam], [self.lam])
        self.ts("dve", self.lam[:, 0:1], self.lam[:, 0:1], lam_init, None, ALU.add, None, [self.lam], [self.lam])
        self.ts("dve", self.lam[:, 1:2], self.lam[:, 0:1], -1.0, None, ALU.mult, None, [self.lam], [self.lam])
        self.bcast_row(self.lnb_rep, i["gmlp_ln_b"].ap()[l:l + 1, :], MW)
        s.dma("sp", self.bs_row[0:1, :], i["gmlp_b_s"].ap()[l:l + 1].rearrange("o g t -> o (g t)"), [],
              [self.bs_row], self.bs_row)
        for g in range(4):
            wn = self.tmpb()
            s.dma("sp", wn[:, 0:128], i["gmlp_w_s"].ap()[l, g], [], [wn], wn)
            self.tt("dve", wn[:, 0:128], wn[:, 0:128], self.gmask[:, :], ALU.mult, [wn, self.gmask], [wn])
            pb = self.bank()
            self.tr([(pb[:, 0:128], wn[:, 0:128], self.identf[:, :])], [wn, self.identf], [pb])
            wt = self.tmpb()
            self.cp("dve", wt[:, 0:128], pb[:, 0:128], [pb], [wt])
            self.cp("dve", self.WsT[:, g, :], pb[:, 0:128], [pb], [self.WsT])
            pb2 = self.bank()
            self.mm([(pb2[:, 0:128], self.lnb_rep[:, g * 128:(g + 1) * 128], wt[:, 0:128], True, False),
                     (pb2[:, 0:128], self.ones_f[0:1, :], self.bs_row[0:1, g * 128:(g + 1) * 128], False, True)],
                    [self.lnb_rep, wt, self.ones_f, self.bs_row], [pb2])
            self.cp("dve", self.Cg[:, g, :], pb2[:, 0:128], [pb2], [self.Cg])


    def alloc_layer_consts(self):
        sb = self.sb
        self.nw_rep = sb("nw_rep", (128, D), F32)
        self.dnw = sb("dnw", (128, 1), F32)
        self.lam = sb("lam", (128, 4), F32)
        self.lamv = sb("lamv", (64, 8), F32)

    def alloc_a_consts(self):
        sb = self.sb
        if True:
            self.lb_rep = sb("lb_rep", (128, MW), F32)
            self.oml_rep = sb("oml_rep", (128, MW), F32)
            self.lb_col = sb("lb_col", (128, 4), F32)
            self.oml_col = sb("oml_col", (128, 4), F32)
            self.lbm1_col = sb("lbm1_col", (128, 4), F32)
            self.hnw = sb("hnw", (128, 1), F32)
            self.lnw_col = sb("lnw_col", (128, 4), F32)
            self.WsT = sb("WsT", (128, 4, 128), BF16)
            self.Cg = sb("Cg", (128, 4, 128), F32)
            self.lnb_rep = sb("lnb_rep", (128, MW), F32)
            self.bs_row = sb("bs_row", (1, MW), F32)

    def rms_stats(self, xt, sm=None):
        sm = sm or self.small
        for b in range(4):
            junk = self.tmpb()
            self.act(junk[:, :], xt[:, b, 0:T], AF.Square, [xt], [junk, sm],
                     accum_out=sm[:, 16 + 2 * b:17 + 2 * b])
            junk2 = self.tmpb()
            self.act(junk2[:, :], xt[:, b, T:D], AF.Square, [xt], [junk2, sm],
                     accum_out=sm[:, 17 + 2 * b:18 + 2 * b])
        smv = sm[:, 16:24].rearrange("p (b two) -> p b two", two=2)
        self.tt("dve", sm[:, 24:28], smv[:, :, 0], smv[:, :, 1], ALU.add, [sm], [sm])
        self.rsqrt(sm[:, 28:32], sm[:, 24:28], [sm], [sm], scale=1.0 / D)

    def norm_part1(self, xt, sm):
        self.rms_stats(xt, sm)
        for b in range(4):
            self.stt("dve", self.hb[:, b, :], xt[:, b, :], sm[:, 28 + b:29 + b], self.nw_rep[:, :], ALU.mult, ALU.mult,
                     [xt, sm, self.nw_rep], [self.hb])

    def rmsnorm_to_hT(self, xt, hT=None):
        self.norm_part1(xt, self.small)
        self.transpose_to(self.hb, hT or self.hT)

    def transpose_to(self, hb, hT):
        for c in range(8):
            pb = self.bank()
            self.mm([(pb[:, b * 128:(b + 1) * 128], hb[:, b, c * 128:(c + 1) * 128], self.ident[:, :], True, True)
                     for b in range(4)], [hb, self.ident], [pb])
            if c % 2 == 0:
                self.act(hT[:, c, :], pb[:, :], AF.Copy, [pb], [hT])
            else:
                self.cp("dve", hT[:, c, :], pb[:, :], [pb], [hT])

    def proj_fm(self, w, col0, hT=None):
        hT = hT or self.hT
        pb = self.bank()
        wv = w[:, :].rearrange("p (c j) -> p c j", c=8)
        self.mm([(pb[:, :], wv[:, c, col0:col0 + 128], hT[:, c, :], c == 0, c == 7) for c in range(8)],
                [w, hT], [pb])
        return pb

    def proj_tm(self, w, b, hT=None):
        hT = hT or self.hT
        pb = self.bank()
        wv = w[:, :].rearrange("p (c j) -> p c j", c=8)
        self.mm([(pb[:, :], hT[:, c, b * 128:(b + 1) * 128], wv[:, c, :], c == 0, c == 7) for c in range(8)],
                [w, hT], [pb])
        return pb

    def phase_a(self, l):
        s, sb = self.s, self.sb
        self.constants_a()
        self.alloc_a_consts()
        self.layer_consts(l)
        if True:
            self.wbr_h = sb("wbr_h", (128, 4, D), BF16)
            self.wbr_g = sb("wbr_g", (128, 4, D), BF16)
            self.f_tok = sb("f_tok", (128, 4, T), F32)
            self.lf = sb("lf", (128, 4, T), F32)
            self.omfT = sb("omfT", (128, 4, T), F32)
            self.v_tok = sb("v_tok", (128, 4, T), BF16)
            self.kdec = sb("kdec", (128, 4, T), BF16)
            self.qs = sb("qs", (128, 4, T), F32)
            self.qtT = sb("qtT", (128, 4, T), BF16)
            self.ktT = sb("ktT", (128, 4, T), BF16)
            self.AT = sb("AT", (128, 4, 128), BF16)
            self.stp = sb("stp", (128, 4, 128), BF16)
            self.o_sb = sb("o_sb", (128, 4, T), F32)
            self.St = sb("St", (128, 4, 128), F32)
            self.ebx = sb("ebx", (128, 32), F32)
            self.sg = sb("sg", (128, 4, T), BF16)
            self.yh = sb("yh", (128, 4, T), BF16)
            self.ug = sb("ug", (128, 4, T), BF16)
            self.vhat = sb("vhat", (128, 4, T), BF16)
            self.yg = sb("yg", (128, 4, T), BF16)
            self.stq = [sb(f"stq{i}", (128, T), BF16) for i in range(6)]
            self.stv = sb("stv", (128, 4, 4, 128), BF16)
            self.stp32 = [sb(f"stp32_{i}", (128, T), F32) for i in range(2)]
            self.bnst = sb("bnst", (128, 8), F32)
            s.op("dve", lambda e: e.memset(self.AT[:, :, :], 0.0), [], [self.AT])
        s.op("dve", lambda e: e.memset(self.St[:, :, :], 0.0), [], [self.St])
        self.bcast_row(self.nw_rep, self.i["norm_mix_w"].ap()[l:l + 1, :], D)
        for tb, n in ((self.wbr_h, "w_br_hgrn"), (self.wbr_g, "w_br_gmlp")):
            s.dma("sp", tb[:, :, :], self.wb[(n, l)].ap()[0].rearrange("p (c j) -> p c j", c=4),
                  [self.wb_b[(n, l)]], [tb], tb)
        xsrc = self.x_in.ap() if l == 0 else self.xs.ap()
        xb = self.x_b if l == 0 else self.xs_b
        stq_i = 0
        stp_i = 0
        from collections import deque
        hT2 = sb("hT2", (128, 8, T), BF16)
        hTs = [self.hT, hT2]
        sms = [self.small, self.small2]

        def load_x(i):
            s.dma("sp", self.xt[:, :, :], xsrc[i * T:(i + 1) * T, :].rearrange("(b p) d -> p b d", p=128), [xb[i]],
                  [self.xt], self.xt)

        load_x(0)
        self.norm_part1(self.xt, sms[0])
        self.transpose_to(self.hb, hTs[0])
        for i in range(self.NT):
            t0 = i * T
            tsl = slice(t0, t0 + T)
            hT = hTs[i % 2]
            more = (i + 1 < self.NT)
            fillers = deque()
            self.run_conv_jobs(2, max_layer=l)

            def fill(n):
                for _ in range(n):
                    if fillers:
                        fillers.popleft()()

            def mk_qk(slot, dst, dbuf, h, st_):
                def f():
                    w = st_["w"] if st_.get("slot") == slot else None
                    if w is None:
                        w = self.load_unit("w_in", l, slot)
                        st_["w"], st_["slot"] = w, slot
                    pb = self.proj_fm(w, h * 128, hT)
                    sq = self.stq[st_["q"] % 6]
                    st_["q"] += 1
                    self.act(sq[:, :], pb[:, :], AF.Copy, [pb], [sq])
                    s.dma("pool", dst.ap()[h, :, tsl], sq[:, :], [sq], [dbuf[i]], sq)
                return f

            def mk_v(b, st_):
                def f():
                    w = st_["w"] if st_.get("slot") == 6 else None
                    if w is None:
                        w = self.load_unit("w_in", l, 6)
                        st_["w"], st_["slot"] = w, 6
                    pb = self.proj_tm(w, b, hT)
                    self.cp("dve", self.stv[:, :, b, :], pb[:, :].rearrange("p (h c) -> p h c", h=4), [pb], [self.stv])
                    if b == 3:
                        s.dma("pool", self.vA.ap()[:, :, 4 * i:4 * i + 4, :].rearrange("h p b c -> p h b c"),
                              self.stv[:, :, :, :], [self.stv], [self.vA_b[i]], self.stv)
                return f

            def mk_ga(dc, st_):
                def f():
                    slot = 11 + dc // 4
                    w = st_["w"] if st_.get("slot") == slot else None
                    if w is None:
                        w = self.load_unit("w_in", l, slot)
                        st_["w"], st_["slot"] = w, slot
                    pb = self.proj_fm(w, (dc % 4) * 128, hT)
                    sq = self.stq[st_["q"] % 6]
                    st_["q"] += 1
                    self.act(sq[:, :], pb[:, :], AF.Sigmoid, [pb], [sq])
                    s.dma("pool", self.gA.ap()[dc, :, tsl], sq[:, :], [sq], [self.gA_b[i]], sq)
                return f

            fst = {"q": stq_i}
            for slot, dst, dbuf in ((4, self.qT, self.qT_b), (5, self.kT, self.kT_b)):
                for h in range(4):
                    fillers.append(mk_qk(slot, dst, dbuf, h, fst))
            for b in range(4):
                fillers.append(mk_v(b, fst))
            for dc in range(8):
                fillers.append(mk_ga(dc, fst))
            w = self.load_unit("w_in", l, 1)
            for b in range(4):
                pb = self.proj_tm(w, b, hT)
                self.act(self.f_tok[:, b, :], pb[:, :], AF.Sigmoid, [pb], [self.f_tok])
            for b in range(4):
                self.tt("dve", self.f_tok[:, b, :], self.f_tok[:, b, :], self.oml_rep[:, :], ALU.mult,
                        [self.f_tok, self.oml_rep], [self.f_tok])
                self.tt("dve", self.f_tok[:, b, :], self.f_tok[:, b, :], self.lb_rep[:, :], ALU.add,
                        [self.f_tok, self.lb_rep], [self.f_tok])
                self.act(self.lf[:, b, :], self.f_tok[:, b, :], AF.Ln, [self.f_tok], [self.lf])
                self.ts("dve", self.f_tok[:, b, :], self.f_tok[:, b, :], -1.0, 1.0, ALU.mult, ALU.add,
                        [self.f_tok], [self.f_tok])
            for h in range(4):
                pb = self.proj_fm(w, h * 128, hT)
                tm = self.tmpb()
                self.act(tm[:, :], pb[:, :], AF.Sigmoid, [pb], [tm])
                self.ts("dve", self.omfT[:, h, :], tm[:, :], self.lbm1_col[:, h:h + 1], self.oml_col[:, h:h + 1],
                        ALU.mult, ALU.add, [tm, self.lbm1_col, self.oml_col], [self.omfT])
            w = self.load_unit("w_in", l, 2)
            for b in range(4):
                pb = self.proj_tm(w, b, hT)
                self.act(self.v_tok[:, b, :], pb[:, :], AF.Copy, [pb], [self.v_tok])
            for b in range(4):
                pb = self.bank()
                self.mm([(pb[:, :], self.TriPP[:, :], self.lf[:, b, :], True, True)], [self.TriPP, self.lf], [pb])
                tm = self.tmpb()
                self.act(tm[:, :], pb[:, :], AF.Exp, [pb], [tm])
                self.tt("pool", self.kdec[:, b, :], self.f_tok[:, b, :], tm[:, :], ALU.mult, [self.f_tok, tm], [self.kdec])
            w = self.load_unit("w_in", l, 0)
            for h in range(4):
                pb = self.proj_fm(w, h * 128, hT)
                self.act(self.qs[:, h, :], pb[:, :], AF.Silu, [pb], [self.qs])
            pbx = self.bank()
            grp = []
            for h in range(4):
                for b in range(4):
                    c0 = (h * 4 + b) * 2
                    grp.append((pbx[:, c0:c0 + 2], self.lf[:, b, h * 128:(h + 1) * 128], self.TriX[:, :], True, True))
            self.mm(grp, [self.lf, self.TriX], [pbx])
            self.act(self.ebx[:, :], pbx[:, 0:32], AF.Exp, [pbx], [self.ebx])
            for h in range(4):
                pb = self.bank()
                self.mm([(pb[:, b * 128:(b + 1) * 128], self.lf[:, b, h * 128:(h + 1) * 128], self.TriP[:, :], True, True)
                         for b in range(4)], [self.lf, self.TriP], [pb])
                tm = self.tmpb()
                self.act(tm[:, :], pb[:, :], AF.Exp, [pb], [tm])
                self.tt("dve", self.qtT[:, h, :], self.qs[:, h, :], tm[:, :], ALU.mult, [self.qs, tm], [self.qtT])
                tm2 = self.tmpb()
                self.act(tm2[:, :], pb[:, :], AF.Exp, [pb], [tm2], scale=-1.0)
                self.tt("pool", self.ktT[:, h, :], self.omfT[:, h, :], tm2[:, :], ALU.mult, [self.omfT, tm2], [self.ktT])
            if more:
                load_x(i + 1)
            for b in range(4):
                bs = slice(b * 128, (b + 1) * 128)
                pb = self.bank()
                b0, b1 = b * 128, b * 128 + 64
                grp = []
                for h in range(4):
                    grp.append((pb[0:64, h * 128:(h + 1) * 128], self.ktT[:, h, b0:b0 + 64], self.qtT[:, h, bs], True, True))
                    grp.append((pb[64:128, h * 128 + 64:(h + 1) * 128], self.ktT[:, h, b1:b1 + 64],
                                self.qtT[:, h, b1:b1 + 64], True, True))
                self.mm(grp, [self.ktT, self.qtT], [pb])
                fill(1)
                s.op("dve", lambda e, pb=pb: e.copy_predicated(
                    self.AT[0:64, :, :], self.hmask[0:64, :, :],
                    pb[0:64, :].rearrange("p (h t) -> p h t", h=4)), [pb, self.hmask], [self.AT])
                s.op("dve", lambda e, pb=pb: e.copy_predicated(
                    self.AT[64:128, :, 64:128], self.hmask[64:128, :, 64:128],
                    pb[64:128, :].rearrange("p (h t) -> p h t", h=4)[:, :, 64:128]), [pb, self.hmask], [self.AT])
                for h in range(4):
                    c63 = (h * 4 + b) * 2
                    self.act(self.stp[:, h, :], self.St[:, h, :], AF.Identity, [self.St, self.ebx], [self.stp],
                             scale=self.ebx[:, c63:c63 + 1])
                po = self.bank()
                grp = []
                for h in range(4):
                    hs = slice(h * 128, (h + 1) * 128)
                    grp.append((po[:, hs], self.v_tok[:, b, hs], self.AT[:, h, :], True, False))
                    grp.append((po[:, hs], self.stp[:, h, :], self.qtT[:, h, bs], False, True))
                fill(1)
                self.mm(grp, [self.v_tok, self.AT, self.stp, self.qtT], [po])
                self.cp("dve", self.o_sb[:, :, bs], po[:, :].rearrange("p (h t) -> p h t", h=4), [po], [self.o_sb])
                pd = self.bank()
                self.mm([(pd[:, h * 128:(h + 1) * 128], self.kdec[:, b, h * 128:(h + 1) * 128],
                          self.v_tok[:, b, h * 128:(h + 1) * 128], True, True) for h in range(4)],
                        [self.kdec, self.v_tok], [pd])
                for h in range(4):
                    c127 = (h * 4 + b) * 2 + 1
                    self.stt("dve", self.St[:, h, :], self.St[:, h, :], self.ebx[:, c127:c127 + 1],
                             pd[:, h * 128:(h + 1) * 128], ALU.mult, ALU.add, [self.St, self.ebx, pd], [self.St])
            w = self.load_unit("w_in", l, 3)
            for h in range(4):
                pb = self.proj_fm(w, h * 128, hT)
                self.act(self.sg[:, h, :], pb[:, :], AF.Silu, [pb], [self.sg])
            sqs, pbs = [], []
            for h in range(4):
                tm = self.tmpb()
                self.act(tm[:, :], self.o_sb[:, h, :], AF.Square, [self.o_sb], [tm])
                sqs.append(tm)
            for h in range(4):
                pb = self.bank()
                self.mm([(pb[:, :], self.ones_m[:, :], sqs[h][:, :], True, True)], [self.ones_m, sqs[h]], [pb])
                pbs.append(pb)
            for h in range(4):
                tm2 = self.tmpb()
                self.rsqrt_el(tm2[:, :], pbs[h][:, :], [pbs[h]], [tm2])
                tm3 = self.tmpb()
                self.stt("dve", tm3[:, :], self.o_sb[:, h, :], self.hnw[:, 0:1], tm2[:, :], ALU.mult, ALU.mult,
                         [self.o_sb, self.hnw, tm2], [tm3])
                self.tt("pool", self.yh[:, h, :], tm3[:, :], self.sg[:, h, :], ALU.mult, [tm3, self.sg], [self.yh])
            if more:
                self.norm_part1(self.xt, sms[(i + 1) % 2])
            w = self.load_unit("w_in", l, 7)
            for g in range(4):
                pb = self.proj_fm(w, g * 128, hT)
                self.act(self.ug[:, g, :], pb[:, :], AF.Gelu, [pb], [self.ug])
            w = self.load_unit("w_in", l, 8)
            for b in range(4):
                pb = self.proj_tm(w, b, hT)
                tm = self.tmpb()
                self.act(tm[:, :], pb[:, :], AF.Gelu, [pb], [tm])
                bn = self.bnst
                s.op("dve", lambda e, tm=tm, bn=bn: e.bn_stats(bn[:, 0:6], tm[:, :]), [tm], [bn])
                s.op("dve", lambda e, bn=bn: e.bn_aggr(bn[:, 6:8], bn[:, 0:6]), [bn], [bn])
                self.rsqrt(bn[:, 7:8], bn[:, 7:8], [bn], [bn])
                self.ts("dve", self.vhat[:, b, :], tm[:, :], bn[:, 6:7], bn[:, 7:8], ALU.subtract, ALU.mult,
                        [tm, bn], [self.vhat])
            fill(4)
            if more:
                self.transpose_to(self.hb, hTs[(i + 1) % 2])
            for g in range(4):
                gs = slice(g * 128, (g + 1) * 128)
                pb = self.bank()
                self.mm([(pb[:, b * 128:(b + 1) * 128], self.vhat[:, b, gs], self.WsT[:, g, :], True, True)
                         for b in range(4)], [self.vhat, self.WsT], [pb])
                tm = self.tmpb()
                for b in range(4):
                    self.stt("dve", tm[:, b * 128:(b + 1) * 128], pb[:, b * 128:(b + 1) * 128],
                             self.lnw_col[:, g:g + 1], self.Cg[:, g, :], ALU.mult, ALU.add,
                             [pb, self.lnw_col, self.Cg], [tm])
                self.tt("dve", self.yg[:, g, :], tm[:, :], self.ug[:, g, :], ALU.mult, [tm, self.ug], [self.yg])
            fill(100)
            for half in range(2):
                wh = self.load_unit("w_in", l, 9 + half)
                for q in range(4):
                    dc = half * 4 + q
                    pb = self.proj_fm(wh, q * 128, hT)
                    gh = self.tmpb()
                    self.act(gh[:, :], pb[:, :], AF.Sigmoid, [pb], [gh])
                    pb2 = self.bank()
                    self.mm([(pb2[:, :], self.wbr_h[:, mc, dc * 128:(dc + 1) * 128], self.yh[:, mc, :], mc == 0, mc == 3)
                             for mc in range(4)], [self.wbr_h, self.yh], [pb2])
                    acc = (self.qs if dc < 4 else self.omfT)
                    self.tt("dve", acc[:, dc % 4, :], pb2[:, :], gh[:, :], ALU.mult, [pb2, gh], [acc])
            for half in range(2):
                wh = self.load_unit("w_in", l, 13 + half)
                for q in range(4):
                    dc = half * 4 + q
                    pb = self.proj_fm(wh, q * 128, hT)
                    gg = self.tmpb()
                    self.act(gg[:, :], pb[:, :], AF.Sigmoid, [pb], [gg])
                    pb2 = self.bank()
                    self.mm([(pb2[:, :], self.wbr_g[:, mc, dc * 128:(dc + 1) * 128], self.yg[:, mc, :], mc == 0, mc == 3)
                             for mc in range(4)], [self.wbr_g, self.yg], [pb2])
                    tm = self.tmpb()
                    self.tt("dve", tm[:, :], pb2[:, :], gg[:, :], ALU.mult, [pb2, gg], [tm])
                    acc = (self.qs if dc < 4 else self.omfT)
                    st = self.stp32[stp_i % 2]
                    stp_i += 1
                    self.tt("pool", st[:, :], acc[:, dc % 4, :], tm[:, :], ALU.add, [acc, tm], [st])
                    s.dma("pool", self.pm.ap()[dc, :, tsl], st[:, :], [st], [self.pm_b[i]], st)
            stq_i = fst["q"]
        self.run_conv_jobs(10 ** 6, max_layer=l)

    def phase_b1(self, l):
        s, sb = self.s, self.sb
        S, NT = self.S, self.NT
        KTh = [sb(f"KTh{j}", (128, S), BF16) for j in range(2)]
        Vh = [sb(f"Vh{j}", (128, S // 128, 128), BF16) for j in range(2)]
        qts = [sb(f"qt{j}", (128, T), BF16) for j in range(3)]
        PT = [sb(f"PT{j}", (128, T), BF16) for j in range(12)]
        sty = [sb(f"sty{j}", (128, T), BF16) for j in range(2)]
        ev = [[sb(f"ev{j}_{k}", (128, T), F32) for k in range(3)] for j in range(2)]
        acc = self.banks[0:3]
        rot = self.banks[3:8]
        NR = len(rot)
        st = {"rot": 0, "pt": 0, "sty": 0}

        def load_head(h):
            s.dma("sp", KTh[h % 2][:, :], self.kT.ap()[h], self.kT_b, [KTh[h % 2]], KTh[h % 2])
            s.dma("sp", Vh[h % 2][:, :, :], self.vA.ap()[h], self.vA_b, [Vh[h % 2]], Vh[h % 2])

        def s_emit(h, i, qt, jb):
            r = jb - 4 * i
            q0 = max(r, 0) * 128
            ks = slice(jb * 128, (jb + 1) * 128)
            pts = []
            for c in range(2):
                ps = rot[st["rot"] % NR]
                st["rot"] += 1
                cs = slice(c * 64, (c + 1) * 64)
                self.mm([(ps[:, q0:T], KTh[h % 2][cs, ks], qt[cs, q0:T], True, True)], [KTh[h % 2], qt], [ps])
                pt = PT[st["pt"] % 12]
                st["pt"] += 1
                if r >= 1:
                    s.op("pool", lambda e, pt=pt, q0=q0: e.memset(pt[:, 0:q0], 0.0), [], [pt])
                self.act(pt[:, q0:T], ps[:, q0:T], AF.Exp, [ps], [pt], scale=0.125)
                if r >= 0:
                    self.tt("dve", pt[:, q0:q0 + 128], pt[:, q0:q0 + 128], self.amask[:, :], ALU.mult,
                            [pt, self.amask], [pt])
                pts.append(pt)
            return pts

        def pv_emit(h, jb, nkb, pts):
            first, last = (jb == 0), (jb == nkb - 1)
            grp = []
            for c in range(2):
                grp.append((acc[c][:, :], Vh[h % 2][:, jb, :], pts[c][:, :], first, last))
            rd = [Vh[h % 2], self.ones_b] + pts
            if jb % 2 == 0:
                st["prev_pts"] = pts
            else:
                pp = st["prev_pts"]
                for t_, ptile in enumerate((pp[0], pp[1], pts[0], pts[1])):
                    grp.append((acc[2][32 * t_:32 * t_ + 32, :], self.ones_b[:, 0:32], ptile[:, :],
                                jb == 1, last, (0, 32 * t_)))
                rd = rd + list(pp)
            self.mm(grp, rd, acc)

        def finalize(h, i, e):
            self.cp("dve", e[0][:, :], acc[0][:, :], [acc[0]], [e[0]])
            self.cp("dve", e[1][:, :], acc[1][:, :], [acc[1]], [e[1]])
            self.cp("dve", e[2][:, :], acc[2][:, :], [acc[2]], [e[2]])

        def finalize2a(h, i, e):
            pa = rot[st["rot"] % NR]
            st["rot"] += 1
            self.mm([(pa[:, :], self.sel1[:, :], e[2][:, :], True, True)], [self.sel1, e[2]], [pa])
            pb2 = rot[st["rot"] % NR]
            st["rot"] += 1
            self.mm([(pb2[:, :], self.sel2[:, :], e[2][:, :], True, True)], [self.sel2, e[2]], [pb2])
            s.op("dve", lambda en: en.reciprocal(e[2][:, :], pa[:, :]), [pa], [e[2]])
            self.tt("dve", e[0][:, :], e[0][:, :], e[2][:, :], ALU.mult, [e[0], e[2]], [e[0]])
            s.op("dve", lambda en: en.reciprocal(e[2][:, :], pb2[:, :]), [pb2], [e[2]])
            self.tt("dve", e[1][:, :], e[1][:, :], e[2][:, :], ALU.mult, [e[1], e[2]], [e[1]])
            self.stt("dve", e[0][:, :], e[1][:, :], self.lam[:, 1:2], e[0][:, :], ALU.mult, ALU.add,
                     [e[0], e[1], self.lam], [e[0]])
            self.tt("dve", e[1][:, :], e[0][:, :], e[0][:, :], ALU.mult, [e[0]], [e[1]])

        def finalize2(h, i, e):
            tsl = slice(i * T, (i + 1) * T)
            pb = rot[st["rot"] % NR]
            st["rot"] += 1
            self.mm([(pb[:, :], self.ones_m[:, :], e[1][:, :], True, True)], [self.ones_m, e[1]], [pb])
            self.rsqrt_el(e[2][:, :], pb[:, :], [pb], [e[2]])
            so = sty[st["sty"] % 2]
            st["sty"] += 1
            self.stt("dve", so[:, :], e[0][:, :], self.dnw[:, 0:1], e[2][:, :], ALU.mult, ALU.mult,
                     [e[0], self.dnw, e[2]], [so])
            s.dma("pool", self.yA.ap()[h, :, tsl], so[:, :], [so], [self.yA_b[i]], so)

        from collections import deque
        yat = sb("yat", (128, 4, T), BF16)
        pmt = sb("pmt", (128, 8, T), F32)
        gat = sb("gat", (128, 8, T), BF16)
        mgT = sb("mgT", (128, 8, T), BF16)
        b2q = deque()

        def rb():
            b_ = rot[st["rot"] % NR]
            st["rot"] += 1
            return b_

        def b2_make(i):
            tsl = slice(i * T, (i + 1) * T)
            steps = []

            def s_load():
                s.dma("sp", yat[:, :, :], self.yA.ap()[:, :, tsl].rearrange("h p t -> p h t"), [self.yA_b[i]], [yat], yat)
                s.dma("sp", pmt[:, :, :], self.pm.ap()[:, :, tsl].rearrange("c p t -> p c t"), [self.pm_b[i]], [pmt], pmt)
                s.dma("sp", gat[:, :, :], self.gA.ap()[:, :, tsl].rearrange("c p t -> p c t"), [self.gA_b[i]], [gat], gat)
            steps.append(s_load)
            steps.extend([None] * 20)
            wst = {}

            def mk_br(dc):
                def f():
                    if dc == 0:
                        wst["w"] = self.load_unit("w_br_attn", l, 0)
                    w = wst["w"]
                    wv = w[:, :].rearrange("p (c j) -> p c j", c=4)
                    pb = rb()
                    self.mm([(pb[:, :], wv[:, mc, dc * 128:(dc + 1) * 128], yat[:, mc, :], mc == 0, mc == 3)
                             for mc in range(4)], [w, yat], [pb])
                    tm = self.tmpb()
                    self.tt("dve", tm[:, :], pb[:, :], gat[:, dc, :], ALU.mult, [pb, gat], [tm])
                    self.tt("pool", mgT[:, dc, :], tm[:, :], pmt[:, dc, :], ALU.add, [tm, pmt], [mgT])
                    if dc == 7:
                        s.dma("pool", self.mg.ap()[:, :, tsl].rearrange("c p t -> p c t"), mgT[:, :, :], [mgT],
                              [self.mg_b[i]], mgT)
                return f

            for dc in range(8):
                steps.append(mk_br(dc))
            return steps

        tiles = [(h, i) for h in range(4) for i in range(NT)]
        load_head(0)

        blocks = []
        for n, (h, i) in enumerate(tiles):
            for jb in range(4 * (i + 1)):
                blocks.append((n, h, i, jb))
        qt_of, pts_of = {}, {}
        sp = [0]
        LOOK = 2

        def emit_s_until(target):
            while sp[0] < min(target, len(blocks)):
                n, h, i, jb = blocks[sp[0]]
                if jb == 0:
                    qt = qts[n % 3]
                    s.dma("sp", qt[:, :], self.qT.ap()[h, :, i * T:(i + 1) * T], [self.qT_b[i]], [qt], qt)
                    qt_of[n] = qt
                pts_of[sp[0]] = s_emit(h, i, qt_of[n], jb)
                sp[0] += 1

        fin2 = None
        for k, (n, h, i, jb) in enumerate(blocks):
            nkb = 4 * (i + 1)
            if jb == 0:
                self.run_conv_jobs(1)
            if jb == 0 and i == 0 and h + 1 < 4:
                load_head(h + 1)
            if jb == 2 and fin2 is not None:
                finalize2a(*fin2)
            if jb == min(7, nkb - 1) and fin2 is not None:
                finalize2(*fin2)
                if fin2[0] == 3:
                    b2q.extend(b2_make(fin2[1]))
                fin2 = None
            emit_s_until(k + 1 + LOOK)
            pv_emit(h, jb, nkb, pts_of.pop(k))
            if b2q:
                f_ = b2q.popleft()
                if f_ is not None:
                    f_()
            if jb == nkb - 1:
                finalize(h, i, ev[n % 2])
                fin2 = (h, i, ev[n % 2])
        finalize2a(*fin2)
        finalize2(*fin2)
        b2q.extend(b2_make(fin2[1]))
        while b2q:
            f_ = b2q.popleft()
            if f_ is not None:
                f_()
        self.run_conv_jobs(10 ** 6)

    def phase_c(self, l):
        s, sb = self.s, self.sb
        if True:
            self.aT = sb("aT", (128, 32, T), BF16)
            self.wff2 = sb("wff2", (128, 32, D), BF16)
        self.bcast_row(self.nw_rep, self.i["norm_ff_w"].ap()[l:l + 1, :], D)
        def load_wff2():
            for u in range(8):
                s.dma("sp", self.wff2[:, u * 4:(u + 1) * 4, :],
                      self.wb[("w_ff2", l)].ap()[u].rearrange("p (c j) -> p c j", c=4),
                      [self.wb_b[("w_ff2", l)]], [self.wff2], self.wff2)
        last = (l == self.L - 1)
        if last:
            fw_rep = sb("fw_rep", (128, D), F32)
            self.bcast_row(fw_rep, self.i["final_norm_w"].ap()[0:1, :], D)
            sm3 = sb("small3", (128, 64), F32)
        xt2 = sb("xt2", (128, 4, D), F32)
        hT2 = sb("hT2c", (128, 8, T), BF16)
        xts = [self.xt, xt2]
        hTs = [self.hT, hT2]
        sms = [self.small, self.small2]
        NT = self.NT

        xsrc = self.x_in.ap() if l == 0 else self.xs.ap()
        xb = self.x_b if l == 0 else self.xs_b
        mgt = sb("mgt", (128, 8, T), BF16)

        def load_x(i):
            xt = xts[i % 2]
            s.dma("sp", xt[:, :, :], xsrc[i * T:(i + 1) * T, :].rearrange("(b p) d -> p b d", p=128),
                  [xb[i]], [xt], xt)
            s.dma("sp", mgt[:, :, :], self.mg.ap()[:, :, i * T:(i + 1) * T].rearrange("c p t -> p c t"),
                  [self.mg_b[i]], [mgt], mgt)

        def outproj(i):
            xt = xts[i % 2]
            for u in range(2):
                w = self.load_unit("w_out", l, u)
                wv = w[:, :].rearrange("p (c j) -> p c j", c=8)
                for b in range(4):
                    pb = self.bank()
                    self.mm([(pb[:, :], mgt[:, dc, b * 128:(b + 1) * 128], wv[:, dc, :], dc == 0, dc == 7)
                             for dc in range(8)], [w, mgt], [pb])
                    self.tt("dve", xt[:, b, u * 512:(u + 1) * 512], xt[:, b, u * 512:(u + 1) * 512], pb[:, :],
                            ALU.add, [xt, pb], [xt])

        load_x(0)
        outproj(0)
        self.norm_part1(xts[0], sms[0])
        self.transpose_to(self.hb, hTs[0])
        for i in range(NT):
            tsl = slice(i * T, (i + 1) * T)
            xt, hT = xts[i % 2], hTs[i % 2]
            more = (i + 1 < NT)
            for u in range(8):
                w = self.load_unit("w_ff1", l, u)
                if i == 0 and u == 0:
                    load_wff2()
                if u == 3 and more:
                    load_x(i + 1)
                for q in range(4):
                    fc = u * 4 + q
                    pb = self.proj_fm(w, q * 128, hT)
                    tm = self.tmpb()
                    self.act(tm[:, :], pb[:, :], AF.Relu, [pb], [tm])
                    self.tt("dve" if fc % 2 == 0 else "pool", self.aT[:, fc, :], tm[:, :], tm[:, :], ALU.mult, [tm], [self.aT])
            if more:
                outproj(i + 1)
                self.norm_part1(xts[(i + 1) % 2], sms[(i + 1) % 2])
            for b in range(4):
                if b == 2 and more:
                    self.transpose_to(self.hb, hTs[(i + 1) % 2])
                for u in range(2):
                    pb = self.bank()
                    self.mm([(pb[:, :], self.aT[:, fc, b * 128:(b + 1) * 128], self.wff2[:, fc, u * 512:(u + 1) * 512],
                              fc == 0, fc == 31) for fc in range(32)], [self.aT, self.wff2], [pb])
                    self.tt("dve", xt[:, b, u * 512:(u + 1) * 512], xt[:, b, u * 512:(u + 1) * 512], pb[:, :],
                            ALU.add, [xt, pb], [xt])
            if last:
                self.rms_stats(xt, sm3)
                for b in range(4):
                    self.stt("dve", xt[:, b, :], xt[:, b, :], sm3[:, 28 + b:29 + b], fw_rep[:, :], ALU.mult, ALU.mult,
                             [xt, sm3, fw_rep], [xt])
                s.dma("pool", self.y_out.ap()[tsl, :].rearrange("(b p) d -> p b d", p=128), xt[:, :, :], [xt],
                      [self.y_b[i]], xt)
            else:
                s.dma("pool", self.xs.ap()[tsl, :].rearrange("(b p) d -> p b d", p=128), xt[:, :, :], [xt],
                      [self.xs_b[i]], xt)


_NAMES = ["norm_mix_w", "w_in", "hgrn_lb_logits", "hgrn_norm_w", "diff_lam_q1", "diff_lam_k1", "diff_lam_q2",
          "diff_lam_k2", "diff_norm_w", "gmlp_ln_w", "gmlp_ln_b", "gmlp_w_s", "gmlp_b_s", "w_br_hgrn", "w_br_attn",
          "w_br_gmlp", "w_out", "norm_ff_w", "w_ff1", "w_ff2"]


def run(inputs, S, n_cores):
    b = Builder(S)
    nc = b.build()
    x = np.ascontiguousarray(np.asarray(inputs["x"], dtype=np.float32))
    common = {n: np.ascontiguousarray(np.asarray(inputs[n], dtype=np.float32)) for n in _NAMES}
    common["final_norm_w"] = np.ascontiguousarray(np.asarray(inputs["final_norm_w"], dtype=np.float32).reshape(1, D))
    in_maps = []
    for c in range(n_cores):
        m = dict(common)
        m["x"] = np.ascontiguousarray(x[c])
        in_maps.append(m)
    res = run_bass_kernel_spmd(nc, in_maps, core_ids=list(range(n_cores)))
    return np.stack([np.asarray(r["y"]) for r in res.results], axis=0).astype(np.float32)


def kernel(**inputs):
    x = np.asarray(inputs["x"])
    return run(inputs, x.shape[1], x.shape[0])
```

```python
import math
from contextlib import ExitStack

import numpy as np
import concourse.bass as bass
import concourse.mybir as mybir
from concourse.bass_utils import run_bass_kernel_spmd

F32 = mybir.dt.float32
BF16 = mybir.dt.bfloat16
U32 = mybir.dt.uint32
AF = mybir.ActivationFunctionType
ALU = mybir.AluOpType
AX = mybir.AxisListType

D = 1024
MW = 512
INC = 7680
DFF = 4096
EPS = 1e-6
T = 512
SEM_LIMIT = 30000


class Tok:
    __slots__ = ("sem", "val", "eng")

    def __init__(self, sem, val, eng):
        self.sem, self.val, self.eng = sem, val, eng


class Buf:
    def __init__(self, name):
        self.name = name
        self.w = None
        self.r = {}
        self.d = {}


class TB(Buf):
    def __init__(self, name, t):
        super().__init__(name)
        self.t = t

    def __getitem__(self, k):
        return self.t[k]


class Sched:
    ENGS = ("pe", "act", "dve", "pool", "sp")

    def __init__(self, nc, stack):
        self.nc, self.stack = nc, stack
        self.ops = {e: [] for e in self.ENGS}
        self.esem = {}
        self.ecnt = {}
        self.known = {e: {} for e in self.ENGS}
        self.nsem = 0
        for e in self.ENGS:
            self._new_esem(e)
        self.dma_bufs = []
        self.free_d = {"hw": [], "sw": []}

    def _sem(self, name):
        self.nsem += 1
        return self.stack.enter_context(self.nc.semaphore(f"{name}_{self.nsem}"))

    def _new_esem(self, e):
        self.esem[e] = self._sem("e" + e)
        self.ecnt[e] = 0

    def _wait(self, eng, tok):
        if tok is None:
            return
        if tok.eng == "pe" and eng == "pe":
            return
        k = self.known[eng]
        key = id(tok.sem)
        if k.get(key, 0) >= tok.val:
            return
        k[key] = tok.val
        self.ops[eng].append(("w", tok.sem, tok.val))

    def _deps(self, eng, reads, writes):
        for b in reads:
            self._wait(eng, b.w)
        for b in writes:
            self._wait(eng, b.w)
            for t in b.r.values():
                self._wait(eng, t)

    def _mark(self, tok, reads, writes):
        for b in reads:
            b.r[id(tok.sem)] = tok
        for b in writes:
            b.w = tok
            b.r = {}

    def op(self, eng, fn, reads=(), writes=()):
        self._deps(eng, reads, writes)
        if self.ecnt[eng] >= SEM_LIMIT:
            self._new_esem(eng)
        self.ecnt[eng] += 1
        tok = Tok(self.esem[eng], self.ecnt[eng], eng)
        self.ops[eng].append(("o", fn, tok.sem, 1))
        self._mark(tok, reads, writes)
        return tok

    def dma(self, q, out_ap, in_ap, reads, writes, owner, serialize=True, **kw):
        self._deps(q, reads, writes)
        kind = "sw" if q == "pool" else "hw"
        st = owner.d.get(kind)
        if st is None:
            st = owner.d[kind] = {"sem": None, "cnt": 0, "last": None}
            self.dma_bufs.append(st)
        if st["sem"] is None and self.free_d[kind] and self.free_d[kind][-1][1] < SEM_LIMIT:
            st["sem"], st["cnt"] = self.free_d[kind].pop()
        if st["sem"] is None or st["cnt"] >= SEM_LIMIT:
            if st["last"] is not None:
                self._wait(q, st["last"])
            st["sem"] = self._sem("d")
            st["cnt"] = 0
        if serialize and st["last"] is not None:
            self._wait(q, st["last"])
        st["cnt"] += 16
        tok = Tok(st["sem"], st["cnt"], "dma")
        st["last"] = tok

        def fn(e, out_ap=out_ap, in_ap=in_ap, kw=kw):
            return e.dma_start(out=out_ap, in_=in_ap, **kw)

        self.ops[q].append(("o", fn, tok.sem, 16))
        self._mark(tok, reads, writes)
        return tok

    def fence(self):
        toks = [Tok(self.esem[e], self.ecnt[e], e) for e in self.ENGS if self.ecnt[e] > 0]
        dt = [st["last"] for st in self.dma_bufs if st["last"] is not None]
        for e in self.ENGS:
            for t in toks:
                if t.eng != e:
                    self._wait(e, t)
            for t in dt:
                self._wait(e, t)

    def retire(self, bufs):
        for b in bufs:
            for kind, st in b.d.items():
                if st["sem"] is not None:
                    self.free_d[kind].append((st["sem"], st["cnt"]))
                if st in self.dma_bufs:
                    self.dma_bufs.remove(st)
            b.d = {}

    def finish(self, q="sp"):
        for st in self.dma_bufs:
            if st["last"] is not None:
                self._wait(q, st["last"])

    def emit(self, block):
        nc = self.nc

        def run(e, lst):
            for it in lst:
                if it[0] == "w":
                    e.wait_ge(it[1], it[2])
                else:
                    ins = it[1](e)
                    ins.then_inc(it[2], it[3])

        @block.tensor
        def _(e):
            run(e, self.ops["pe"])

        @block.scalar
        def _(e):
            run(e, self.ops["act"])

        @block.vector
        def _(e):
            run(e, self.ops["dve"])

        @block.gpsimd
        def _(e):
            run(e, self.ops["pool"])

        @block.sync
        def _(e):
            run(e, self.ops["sp"])


class Builder:
    def __init__(self, S, n_layers=2, debug_out=None):
        self.S = S
        self.NT = S // T
        self.L = n_layers
        self.debug_out = debug_out
        self.nc = bass.Bass("TRN2", target_bir_lowering=False)
        self.sbuf_bytes = 0

    def sb(self, name, shape, dtype):
        self.sb_n = getattr(self, "sb_n", 0) + 1
        name = f"{name}_{self.sb_n}"
        t = self.stack.enter_context(self.nc.sbuf_tensor(name, list(shape), dtype))
        if getattr(self, "phase_tbs", None) is not None:
            tb_ = TB(name, t)
            self.phase_tbs.append(tb_)
            return tb_
        n = 1
        for s in shape[1:]:
            n *= s
        self.sbuf_bytes += n * (2 if dtype == BF16 else 4)
        return TB(name, t)

    def dram(self, name, shape, dtype, kind="Internal"):
        return self.nc.dram_tensor(name, list(shape), dtype, kind=kind)

    def mm(self, group, reads, writes):
        def fn(e, group=group):
            ins = None
            for g in group:
                o, l, r, st, sp = g[:5]
                if len(g) > 5:
                    ins = e.matmul(o, l, r, start=st, stop=sp, tile_position=g[5])
                else:
                    ins = e.matmul(o, l, r, start=st, stop=sp)
            return ins
        return self.s.op("pe", fn, reads, writes)

    def tr(self, group, reads, writes):
        def fn(e, group=group):
            ins = None
            for (o, i, idn) in group:
                ins = e.transpose(o, i, idn)
            return ins
        return self.s.op("pe", fn, reads, writes)

    def act(self, out, in_, func, reads, writes, **kw):
        def fn(e):
            return e.activation(out, in_, func, **kw)
        return self.s.op("act", fn, reads, writes)

    def tt(self, eng, out, in0, in1, op, reads, writes):
        def fn(e):
            return e.tensor_tensor(out, in0, in1, op)
        return self.s.op(eng, fn, reads, writes)

    def ts(self, eng, out, in0, s1, s2, op0, op1, reads, writes):
        def fn(e):
            if s2 is None:
                return e.tensor_scalar(out, in0, s1, None, op0)
            return e.tensor_scalar(out, in0, s1, s2, op0, op1)
        return self.s.op(eng, fn, reads, writes)

    def stt(self, eng, out, in0, scalar, in1, op0, op1, reads, writes):
        def fn(e):
            return e.scalar_tensor_tensor(out, in0, scalar, in1, op0, op1)
        return self.s.op(eng, fn, reads, writes)

    def rsqrt(self, out, in_, reads, writes, scale=1.0):
        self.act(out, in_, AF.Sqrt, reads, writes, bias=self.eps_col[:, 0:1], scale=scale)
        self.s.op("dve", lambda e: e.reciprocal(out, out), writes, writes)

    def rsqrt_el(self, out, in_, reads, writes):
        self.act(out, in_, AF.Ln, reads, writes, bias=self.eps_col[:, 0:1], scale=1.0)
        self.act(out, out, AF.Exp, writes, writes, scale=-0.5)

    def cp(self, eng, out, in_, reads, writes):
        def fn(e):
            return e.tensor_copy(out, in_)
        return self.s.op(eng, fn, reads, writes)

    def bank(self):
        b = self.banks[self.bank_i % len(self.banks)]
        self.bank_i += 1
        return b

    def build(self):
        nc = self.nc
        S, L = self.S, self.L
        with ExitStack() as stack:
            self.stack = stack
            self.s = Sched(nc, stack)
            self.declare_io()
            self.alloc_common()
            self.constants()
            self.convert_weights()
            self.alloc_layer_consts()
            self.flush()
            outer = stack
            for l in range(L):
                for ph in (self.phase_a, self.phase_b1, self.phase_c):
                    with ExitStack() as ps:
                        self.stack = ps
                        self.phase_tbs = []
                        ph(l)
                        self.flush()
                        self.s.retire(self.phase_tbs)
                        self.phase_tbs = None
                    self.stack = outer
            self.s.finish("sp")
            self.flush()
        return nc

    def flush(self):
        self.s.fence()
        with self.nc.Block() as block:
            self.s.emit(block)
        self.s.ops = {e: [] for e in Sched.ENGS}

    def declare_io(self):
        S, L = self.S, self.L
        ein = lambda n, sh: self.nc.dram_tensor(n, list(sh), F32, kind="ExternalInput")
        self.x_in = ein("x", (S, D))
        self.i = {}
        for n, sh in [("norm_mix_w", (2, D)), ("w_in", (2, D, INC)), ("hgrn_lb_logits", (2, MW)),
                      ("hgrn_norm_w", (2, 128)), ("diff_lam_q1", (2, 64)), ("diff_lam_k1", (2, 64)),
                      ("diff_lam_q2", (2, 64)), ("diff_lam_k2", (2, 64)), ("diff_norm_w", (2, 128)),
                      ("gmlp_ln_w", (2, MW)), ("gmlp_ln_b", (2, MW)), ("gmlp_w_s", (2, 4, 128, 128)),
                      ("gmlp_b_s", (2, 4, 128)), ("w_br_hgrn", (2, MW, D)), ("w_br_attn", (2, MW, D)),
                      ("w_br_gmlp", (2, MW, D)), ("w_out", (2, D, D)), ("norm_ff_w", (2, D)),
                      ("w_ff1", (2, D, DFF)), ("w_ff2", (2, DFF, D)), ("final_norm_w", (1, D))]:
            self.i[n] = ein(n, sh)
        self.y_out = self.nc.dram_tensor("y", [S, D], F32, kind="ExternalOutput")
        NT = self.NT
        self.xs = self.dram("xs", (S, D), F32)
        self.xs_b = [Buf(f"xs{i}") for i in range(NT)]
        self.x_b = [Buf(f"xin{i}") for i in range(NT)]
        self.y_b = [Buf(f"y{i}") for i in range(NT)]
        self.qT = self.dram("qT", (4, 128, S), BF16)
        self.kT = self.dram("kT", (4, 128, S), BF16)
        self.vA = self.dram("vA", (4, 128, S // 128, 128), BF16)
        self.gA = self.dram("gA", (8, 128, S), BF16)
        self.pm = self.dram("pm", (8, 128, S), F32)
        self.yA = self.dram("yA", (4, 128, S), BF16)
        self.mg = self.dram("mg", (8, 128, S), BF16)
        self.mg_b = [Buf(f"mg{i}") for i in range(NT)]
        mk = lambda n: [Buf(f"{n}{i}") for i in range(NT)]
        self.qT_b, self.kT_b, self.vA_b, self.gA_b, self.pm_b, self.yA_b = (
            mk("qT"), mk("kT"), mk("vA"), mk("gA"), mk("pm"), mk("yA"))
        self.wb = {}
        self.wb_b = {}
        self.wb_u = {("w_in", 0, u): Buf(f"wb_w_in_0_{u}") for u in range(15)}
        for l in range(L):
            for n, nu in [("w_in", 15), ("w_br_hgrn", 1), ("w_br_attn", 1), ("w_br_gmlp", 1),
                          ("w_out", 2), ("w_ff1", 8), ("w_ff2", 8)]:
                self.wb[(n, l)] = self.dram(f"wb_{n}_{l}", (nu, 128, 4096), BF16)
                self.wb_b[(n, l)] = Buf(f"wb_{n}_{l}")

    def alloc_common(self):
        nc = self.nc
        self.banks = []
        for i in range(8):
            t = self.stack.enter_context(nc.psum_tensor(f"bank{i}", [128, 512], F32))
            self.banks.append(TB(f"bank{i}", t))
        self.bank_i = 0
        sb = self.sb
        self.ring = [sb(f"ring{i}", (128, 4096), BF16) for i in range(3)]
        self.ring_i = 0
        self.xt = sb("xt", (128, 4, D), F32)
        self.hb = sb("hb", (128, 4, D), BF16)
        self.hT = sb("hT", (128, 8, T), BF16)
        self.tmp = [sb(f"tmp{i}", (128, T), F32) for i in range(4)]
        self.tmp_i = 0
        self.small = sb("small", (128, 64), F32)
        self.small2 = sb("small2", (128, 64), F32)

    def tmpb(self):
        t = self.tmp[self.tmp_i % 4]
        self.tmp_i += 1
        return t

    def run_conv_jobs(self, n, max_layer=None):
        for _ in range(n):
            if self.conv_jobs and (max_layer is None or self.conv_jobs[0][0] <= max_layer):
                self.conv_jobs.pop(0)[1]()

    def load_unit(self, name, l, u):
        r = self.ring[self.ring_i % 3]
        self.ring_i += 1
        dep = self.wb_u.get((name, l, u), self.wb_b[(name, l)])
        self.s.dma("sp", r[:, :], self.wb[(name, l)].ap()[u], [dep], [r], r)
        return r

    def bcast_row(self, dst, src_row_ap, n):
        self.s.dma("sp", dst[:, 0:n], src_row_ap.partition_broadcast(128), [], [dst], dst)

    def constants(self):
        nc, s, sb = self.nc, self.s, self.sb
        self.J = sb("iotaJ", (128, 128), F32)
        self.P = sb("iotaP", (128, 128), F32)
        s.op("pool", lambda e: e.iota(self.J[:, :], [[1, 128]], base=0, channel_multiplier=0,
                                      allow_small_or_imprecise_dtypes=True), [], [self.J])
        s.op("pool", lambda e: e.iota(self.P[:, :], [[0, 128]], base=0, channel_multiplier=1,
                                      allow_small_or_imprecise_dtypes=True), [], [self.P])
        J, P = self.J, self.P
        self.ident = sb("ident", (128, 128), BF16)
        self.tt("dve", self.ident[:, :], J[:, :], P[:, :], ALU.is_equal, [J, P], [self.ident])
        self.amask = sb("amask", (128, 128), BF16)
        self.tt("dve", self.amask[:, :], P[:, :], J[:, :], ALU.is_le, [J, P], [self.amask])
        self.ones_m = sb("ones_m", (128, 128), F32)
        s.op("dve", lambda e: e.memset(self.ones_m[:, :], 1.0 / 128.0), [], [self.ones_m])
        self.ones_b = sb("ones_b", (128, 128), BF16)
        s.op("dve", lambda e: e.memset(self.ones_b[:, :], 1.0), [], [self.ones_b])
        self.sel1 = sb("sel1", (128, 128), F32)
        self.sel2 = sb("sel2", (128, 128), F32)
        t0 = self.tmpb()
        self.ts("dve", self.sel1[:, :], P[:, :], 31.0, 1.0 / 32.0, ALU.is_le, ALU.mult, [P], [self.sel1])
        self.ts("dve", t0[:, 0:128], P[:, :], 64.0, 1.0 / 32.0, ALU.is_ge, ALU.mult, [P], [t0])
        self.ts("dve", t0[:, 128:256], P[:, :], 95.0, None, ALU.is_le, None, [P], [t0])
        self.tt("dve", t0[:, 0:128], t0[:, 0:128], t0[:, 128:256], ALU.mult, [t0], [t0])
        self.tt("dve", self.sel1[:, :], self.sel1[:, :], t0[:, 0:128], ALU.add, [self.sel1, t0], [self.sel1])
        self.ts("dve", self.sel2[:, :], self.sel1[:, :], -1.0, 1.0 / 32.0, ALU.mult, ALU.add, [self.sel1], [self.sel2])
        self.eps_col = sb("eps_col", (128, 1), F32)
        s.op("dve", lambda e: e.memset(self.eps_col[:, :], EPS), [], [self.eps_col])
        self.ones_f = sb("ones_f", (128, 128), F32)
        s.op("dve", lambda e: e.memset(self.ones_f[:, :], 1.0), [], [self.ones_f])

    def constants_a(self):
        nc, s, sb = self.nc, self.s, self.sb
        J, P = self.J, self.P
        self.identf = sb("identf", (128, 128), F32)
        self.tt("dve", self.identf[:, :], J[:, :], P[:, :], ALU.is_equal, [J, P], [self.identf])
        self.TriP = sb("TriP", (128, 128), F32)
        t0 = self.tmpb()
        self.tt("dve", self.TriP[:, :], P[:, :], J[:, :], ALU.is_le, [J, P], [self.TriP])
        self.ts("dve", t0[:, 0:128], P[:, :], 63.0, None, ALU.is_le, None, [P], [t0])
        self.tt("dve", self.TriP[:, :], self.TriP[:, :], t0[:, 0:128], ALU.subtract, [self.TriP, t0], [self.TriP])
        self.TriPP = sb("TriPP", (128, 128), F32)
        self.tt("dve", self.TriPP[:, :], P[:, :], J[:, :], ALU.is_gt, [J, P], [self.TriPP])
        self.TriX = sb("TriX", (128, 2), F32)
        self.ts("dve", self.TriX[:, 0:1], P[:, 0:1], 63.0, None, ALU.is_le, None, [P], [self.TriX])
        self.ts("dve", self.TriX[:, 1:2], P[:, 0:1], 0.0, None, ALU.is_ge, None, [P], [self.TriX])
        self.hmask = sb("hmask", (128, 4, 128), U32)
        for h in range(4):
            self.tt("dve", self.hmask[:, h, :], P[:, :], J[:, :], ALU.is_le, [J, P], [self.hmask])
        self.gmask = sb("gmask", (128, 128), F32)
        self.tt("dve", self.gmask[:, :], J[:, :], P[:, :], ALU.is_le, [J, P], [self.gmask])

    def convert_weights(self):
        s = self.s
        self.conv_jobs = []
        for l in range(self.L):
            def conv(name, src_units, l=l):
                dst = self.wb[(name, l)].ap()
                order = list(range(len(src_units)))
                if name == "w_in":
                    order = [1, 2, 0, 4, 5, 3, 7, 8, 6, 11, 12, 9, 10, 13, 14]
                for u in order:
                    src = src_units[u]
                    owner = self.wb_u.get((name, l, u), self.wb_b[(name, l)])

                    def job(u=u, src=src, owner=owner, dst=dst):
                        s.dma("pool", dst[u].rearrange("p (c j) -> p c j", c=src.shape[1]), src, [], [owner],
                              owner, serialize=False)
                    if l == 0 and name in ("w_in", "w_br_hgrn", "w_br_gmlp"):
                        job()
                    else:
                        self.conv_jobs.append((l, job))
            w = self.i["w_in"].ap()[l]
            wv = w.rearrange("(c p) (u j) -> u p c j", p=128, j=512)
            conv("w_in", [wv[u] for u in range(15)])
            for n in ("w_br_hgrn", "w_br_attn", "w_br_gmlp"):
                w = self.i[n].ap()[l]
                conv(n, [w.rearrange("(c p) j -> p c j", p=128)])
            w = self.i["w_out"].ap()[l]
            wv = w.rearrange("(c p) (u j) -> u p c j", p=128, j=512)
            conv("w_out", [wv[u] for u in range(2)])
            w = self.i["w_ff1"].ap()[l]
            wv = w.rearrange("(c p) (u j) -> u p c j", p=128, j=512)
            conv("w_ff1", [wv[u] for u in range(8)])
            w = self.i["w_ff2"].ap()[l]
            wv = w.rearrange("(u c p) j -> u p c j", p=128, c=4)
            conv("w_ff2", [wv[u] for u in range(8)])

    def layer_consts(self, l):
        s, sb, i = self.s, self.sb, self.i
        lg = i["hgrn_lb_logits"].ap()
        if l == 0:
            s.op("dve", lambda e: e.memset(self.lb_rep[:, :], 0.0), [], [self.lb_rep])
            s.op("dve", lambda e: e.memset(self.lb_col[:, :], 0.0), [], [self.lb_col])
        else:
            t0, t1 = self.tmpb(), self.tmpb()
            self.bcast_row(t0, lg[0:1, :], MW)
            self.bcast_row(t1, lg[1:2, :], MW)
            self.tt("dve", t1[:, :], t1[:, :], t0[:, :], ALU.subtract, [t0, t1], [t1])
            self.act(self.lb_rep[:, :], t1[:, :], AF.Sigmoid, [t1], [self.lb_rep])
            sm = self.small
            s.dma("sp", sm[:, 0:4], lg[0].rearrange("(h p) -> p h", p=128), [], [sm], sm,
                  allow_slow_non_contiguous=True)
            s.dma("sp", sm[:, 4:8], lg[1].rearrange("(h p) -> p h", p=128), [], [sm], sm,
                  allow_slow_non_contiguous=True)
            self.tt("dve", sm[:, 8:12], sm[:, 4:8], sm[:, 0:4], ALU.subtract, [sm], [sm])
            self.act(self.lb_col[:, :], sm[:, 8:12], AF.Sigmoid, [sm], [self.lb_col])
        self.ts("dve", self.oml_rep[:, :], self.lb_rep[:, :], -1.0, 1.0, ALU.mult, ALU.add, [self.lb_rep], [self.oml_rep])
        self.ts("dve", self.oml_col[:, :], self.lb_col[:, :], -1.0, 1.0, ALU.mult, ALU.add, [self.lb_col], [self.oml_col])
        self.ts("dve", self.lbm1_col[:, :], self.lb_col[:, :], -1.0, None, ALU.add, None, [self.lb_col], [self.lbm1_col])
        s.dma("sp", self.hnw[:, 0:1], i["hgrn_norm_w"].ap()[l].rearrange("(p o) -> p o", o=1), [], [self.hnw], self.hnw,
              allow_slow_non_contiguous=True)
        s.dma("sp", self.dnw[:, 0:1], i["diff_norm_w"].ap()[l].rearrange("(p o) -> p o", o=1), [], [self.dnw], self.dnw,
              allow_slow_non_contiguous=True)
        lam_init = 0.8 - 0.6 * math.exp(-0.3 * l)
        self.ts("dve", self.dnw[:, :], self.dnw[:, :], 1.0 - lam_init, None, ALU.mult, None, [self.dnw], [self.dnw])
        s.dma("sp", self.lnw_col[:, :], i["gmlp_ln_w"].ap()[l].rearrange("(g p) -> p g", p=128), [], [self.lnw_col],
              self.lnw_col, allow_slow_non_contiguous=True)
        lv = self.lamv
        for j, n in enumerate(("diff_lam_q1", "diff_lam_k1", "diff_lam_q2", "diff_lam_k2")):
            s.dma("sp", lv[:, j:j + 1], i[n].ap()[l].rearrange("(p o) -> p o", o=1), [], [lv], lv,
                  allow_slow_non_contiguous=True)
        self.tt("dve", lv[:, 4:5], lv[:, 0:1], lv[:, 1:2], ALU.mult, [lv], [lv])
        self.tt("dve", lv[:, 5:6], lv[:, 2:3], lv[:, 3:4], ALU.mult, [lv], [lv])
        pb = self.bank()
        self.mm([(pb[:, 0:2], self.ones_f[0:64, :], lv[:, 4:6], True, True)], [self.ones_f, lv], [pb])
        self.act(self.lam[:, 2:4], pb[:, 0:2], AF.Exp, [pb], [self.lam])
        self.tt("dve", self.lam[:, 0:1], self.lam[:, 2:3], self.lam[:, 3:4], ALU.subtract, [self.lam], [self.lam])
        self.ts("dve", self.lam[:, 0:1], self.lam[:, 0:1], lam_init, None, ALU.add, None, [self.lam], [self.lam])
        self.ts("dve", self.lam[:, 1:2], self.lam[:, 0:1], -1.0, None, ALU.mult, None, [self.lam], [self.lam])
        self.bcast_row(self.lnb_rep, i["gmlp_ln_b"].ap()[l:l + 1, :], MW)
        s.dma("sp", self.bs_row[0:1, :], i["gmlp_b_s"].ap()[l:l + 1].rearrange("o g t -> o (g t)"), [],
              [self.bs_row], self.bs_row)
        for g in range(4):
            wn = self.tmpb()
            s.dma("sp", wn[:, 0:128], i["gmlp_w_s"].ap()[l, g], [], [wn], wn)
            self.tt("dve", wn[:, 0:128], wn[:, 0:128], self.gmask[:, :], ALU.mult, [wn, self.gmask], [wn])
            pb = self.bank()
            self.tr([(pb[:, 0:128], wn[:, 0:128], self.identf[:, :])], [wn, self.identf], [pb])
            wt = self.tmpb()
            self.cp("dve", wt[:, 0:128], pb[:, 0:128], [pb], [wt])
            self.cp("dve", self.WsT[:, g, :], pb[:, 0:128], [pb], [self.WsT])
            pb2 = self.bank()
            self.mm([(pb2[:, 0:128], self.lnb_rep[:, g * 128:(g + 1) * 128], wt[:, 0:128], True, False),
                     (pb2[:, 0:128], self.ones_f[0:1, :], self.bs_row[0:1, g * 128:(g + 1) * 128], False, True)],
                    [self.lnb_rep, wt, self.ones_f, self.bs_row], [pb2])
            self.cp("dve", self.Cg[:, g, :], pb2[:, 0:128], [pb2], [self.Cg])


    def alloc_layer_consts(self):
        sb = self.sb
        self.nw_rep = sb("nw_rep", (128, D), F32)
        self.dnw = sb("dnw", (128, 1), F32)
        self.lam = sb("lam", (128, 4), F32)
        self.lamv = sb("lamv", (64, 8), F32)

    def alloc_a_consts(self):
        sb = self.sb
        if True:
            self.lb_rep = sb("lb_rep", (128, MW), F32)
            self.oml_rep = sb("oml_rep", (128, MW), F32)
            self.lb_col = sb("lb_col", (128, 4), F32)
            self.oml_col = sb("oml_col", (128, 4), F32)
            self.lbm1_col = sb("lbm1_col", (128, 4), F32)
            self.hnw = sb("hnw", (128, 1), F32)
            self.lnw_col = sb("lnw_col", (128, 4), F32)
            self.WsT = sb("WsT", (128, 4, 128), BF16)
            self.Cg = sb("Cg", (128, 4, 128), F32)
            self.lnb_rep = sb("lnb_rep", (128, MW), F32)
            self.bs_row = sb("bs_row", (1, MW), F32)

    def rms_stats(self, xt, sm=None):
        sm = sm or self.small
        for b in range(4):
            junk = self.tmpb()
            self.act(junk[:, :], xt[:, b, 0:T], AF.Square, [xt], [junk, sm],
                     accum_out=sm[:, 16 + 2 * b:17 + 2 * b])
            junk2 = self.tmpb()
            self.act(junk2[:, :], xt[:, b, T:D], AF.Square, [xt], [junk2, sm],
                     accum_out=sm[:, 17 + 2 * b:18 + 2 * b])
        smv = sm[:, 16:24].rearrange("p (b two) -> p b two", two=2)
        self.tt("dve", sm[:, 24:28], smv[:, :, 0], smv[:, :, 1], ALU.add, [sm], [sm])
        self.rsqrt(sm[:, 28:32], sm[:, 24:28], [sm], [sm], scale=1.0 / D)

    def norm_part1(self, xt, sm):
        self.rms_stats(xt, sm)
        for b in range(4):
            self.stt("dve", self.hb[:, b, :], xt[:, b, :], sm[:, 28 + b:29 + b], self.nw_rep[:, :], ALU.mult, ALU.mult,
                     [xt, sm, self.nw_rep], [self.hb])

    def rmsnorm_to_hT(self, xt, hT=None):
        self.norm_part1(xt, self.small)
        self.transpose_to(self.hb, hT or self.hT)

    def transpose_to(self, hb, hT):
        for c in range(8):
            pb = self.bank()
            self.mm([(pb[:, b * 128:(b + 1) * 128], hb[:, b, c * 128:(c + 1) * 128], self.ident[:, :], True, True)
                     for b in range(4)], [hb, self.ident], [pb])
            if c % 2 == 0:
                self.act(hT[:, c, :], pb[:, :], AF.Copy, [pb], [hT])
            else:
                self.cp("dve", hT[:, c, :], pb[:, :], [pb], [hT])

    def proj_fm(self, w, col0, hT=None):
        hT = hT or self.hT
        pb = self.bank()
        wv = w[:, :].rearrange("p (c j) -> p c j", c=8)
        self.mm([(pb[:, :], wv[:, c, col0:col0 + 128], hT[:, c, :], c == 0, c == 7) for c in range(8)],
                [w, hT], [pb])
        return pb

    def proj_tm(self, w, b, hT=None):
        hT = hT or self.hT
        pb = self.bank()
        wv = w[:, :].rearrange("p (c j) -> p c j", c=8)
        self.mm([(pb[:, :], hT[:, c, b * 128:(b + 1) * 128], wv[:, c, :], c == 0, c == 7) for c in range(8)],
                [w, hT], [pb])
        return pb

    def phase_a(self, l):
        s, sb = self.s, self.sb
        self.constants_a()
        self.alloc_a_consts()
        self.layer_consts(l)
        if True:
            self.wbr_h = sb("wbr_h", (128, 4, D), BF16)
            self.wbr_g = sb("wbr_g", (128, 4, D), BF16)
            self.f_tok = sb("f_tok", (128, 4, T), F32)
            self.lf = sb("lf", (128, 4, T), F32)
            self.omfT = sb("omfT", (128, 4, T), F32)
            self.v_tok = sb("v_tok", (128, 4, T), BF16)
            self.kdec = sb("kdec", (128, 4, T), BF16)
            self.qs = sb("qs", (128, 4, T), F32)
            self.qtT = sb("qtT", (128, 4, T), BF16)
            self.ktT = sb("ktT", (128, 4, T), BF16)
            self.AT = sb("AT", (128, 4, 128), BF16)
            self.stp = sb("stp", (128, 4, 128), BF16)
            self.o_sb = sb("o_sb", (128, 4, T), F32)
            self.St = sb("St", (128, 4, 128), F32)
            self.ebx = sb("ebx", (128, 32), F32)
            self.sg = sb("sg", (128, 4, T), BF16)
            self.yh = sb("yh", (128, 4, T), BF16)
            self.ug = sb("ug", (128, 4, T), BF16)
            self.vhat = sb("vhat", (128, 4, T), BF16)
            self.yg = sb("yg", (128, 4, T), BF16)
            self.stq = [sb(f"stq{i}", (128, T), BF16) for i in range(6)]
            self.stv = sb("stv", (128, 4, 4, 128), BF16)
            self.stp32 = [sb(f"stp32_{i}", (128, T), F32) for i in range(2)]
            self.bnst = sb("bnst", (128, 8), F32)
            s.op("dve", lambda e: e.memset(self.AT[:, :, :], 0.0), [], [self.AT])
        s.op("dve", lambda e: e.memset(self.St[:, :, :], 0.0), [], [self.St])
        self.bcast_row(self.nw_rep, self.i["norm_mix_w"].ap()[l:l + 1, :], D)
        def load_wbr():
            for tb, n in ((self.wbr_h, "w_br_hgrn"), (self.wbr_g, "w_br_gmlp")):
                s.dma("sp", tb[:, :, :], self.wb[(n, l)].ap()[0].rearrange("p (c j) -> p c j", c=4),
                      [self.wb_b[(n, l)]], [tb], tb)
        xsrc = self.x_in.ap() if l == 0 else self.xs.ap()
        xb = self.x_b if l == 0 else self.xs_b
        stq_i = 0
        stp_i = 0
        from collections import deque
        hT2 = sb("hT2", (128, 8, T), BF16)
        hTs = [self.hT, hT2]
        sms = [self.small, self.small2]

        def load_x(i):
            s.dma("sp", self.xt[:, :, :], xsrc[i * T:(i + 1) * T, :].rearrange("(b p) d -> p b d", p=128), [xb[i]],
                  [self.xt], self.xt)

        load_x(0)
        self.norm_part1(self.xt, sms[0])
        self.transpose_to(self.hb, hTs[0])
        for i in range(self.NT):
            t0 = i * T
            tsl = slice(t0, t0 + T)
            hT = hTs[i % 2]
            more = (i + 1 < self.NT)
            fillers = deque()
            self.run_conv_jobs(2, max_layer=l)

            def fill(n):
                for _ in range(n):
                    if fillers:
                        fillers.popleft()()

            def mk_qk(slot, dst, dbuf, h, st_):
                def f():
                    w = st_["w"] if st_.get("slot") == slot else None
                    if w is None:
                        w = self.load_unit("w_in", l, slot)
                        st_["w"], st_["slot"] = w, slot
                    pb = self.proj_fm(w, h * 128, hT)
                    sq = self.stq[st_["q"] % 6]
                    st_["q"] += 1
                    self.act(sq[:, :], pb[:, :], AF.Copy, [pb], [sq])
                    s.dma("pool", dst.ap()[h, :, tsl], sq[:, :], [sq], [dbuf[i]], sq)
                return f

            def mk_v(b, st_):
                def f():
                    w = st_["w"] if st_.get("slot") == 6 else None
                    if w is None:
                        w = self.load_unit("w_in", l, 6)
                        st_["w"], st_["slot"] = w, 6
                    pb = self.proj_tm(w, b, hT)
                    self.cp("dve", self.stv[:, :, b, :], pb[:, :].rearrange("p (h c) -> p h c", h=4), [pb], [self.stv])
                    if b == 3:
                        s.dma("pool", self.vA.ap()[:, :, 4 * i:4 * i + 4, :].rearrange("h p b c -> p h b c"),
                              self.stv[:, :, :, :], [self.stv], [self.vA_b[i]], self.stv)
                return f

            def mk_ga(dc, st_):
                def f():
                    slot = 11 + dc // 4
                    w = st_["w"] if st_.get("slot") == slot else None
                    if w is None:
                        w = self.load_unit("w_in", l, slot)
                        st_["w"], st_["slot"] = w, slot
                    pb = self.proj_fm(w, (dc % 4) * 128, hT)
                    sq = self.stq[st_["q"] % 6]
                    st_["q"] += 1
                    self.act(sq[:, :], pb[:, :], AF.Sigmoid, [pb], [sq])
                    s.dma("pool", self.gA.ap()[dc, :, tsl], sq[:, :], [sq], [self.gA_b[i]], sq)
                return f

            fst = {"q": stq_i}
            for slot, dst, dbuf in ((4, self.qT, self.qT_b), (5, self.kT, self.kT_b)):
                for h in range(4):
                    fillers.append(mk_qk(slot, dst, dbuf, h, fst))
            for b in range(4):
                fillers.append(mk_v(b, fst))
            for dc in range(8):
                fillers.append(mk_ga(dc, fst))
            w = self.load_unit("w_in", l, 1)
            for b in range(4):
                pb = self.proj_tm(w, b, hT)
                self.act(self.f_tok[:, b, :], pb[:, :], AF.Sigmoid, [pb], [self.f_tok])
            for b in range(4):
                self.tt("dve", self.f_tok[:, b, :], self.f_tok[:, b, :], self.oml_rep[:, :], ALU.mult,
                        [self.f_tok, self.oml_rep], [self.f_tok])
                self.tt("dve", self.f_tok[:, b, :], self.f_tok[:, b, :], self.lb_rep[:, :], ALU.add,
                        [self.f_tok, self.lb_rep], [self.f_tok])
                self.act(self.lf[:, b, :], self.f_tok[:, b, :], AF.Ln, [self.f_tok], [self.lf])
                self.ts("dve", self.f_tok[:, b, :], self.f_tok[:, b, :], -1.0, 1.0, ALU.mult, ALU.add,
                        [self.f_tok], [self.f_tok])
            for h in range(4):
                pb = self.proj_fm(w, h * 128, hT)
                tm = self.tmpb()
                self.act(tm[:, :], pb[:, :], AF.Sigmoid, [pb], [tm])
                self.ts("dve", self.omfT[:, h, :], tm[:, :], self.lbm1_col[:, h:h + 1], self.oml_col[:, h:h + 1],
                        ALU.mult, ALU.add, [tm, self.lbm1_col, self.oml_col], [self.omfT])
            w = self.load_unit("w_in", l, 2)
            for b in range(4):
                pb = self.proj_tm(w, b, hT)
                self.act(self.v_tok[:, b, :], pb[:, :], AF.Copy, [pb], [self.v_tok])
            for b in range(4):
                pb = self.bank()
                self.mm([(pb[:, :], self.TriPP[:, :], self.lf[:, b, :], True, True)], [self.TriPP, self.lf], [pb])
                tm = self.tmpb()
                self.act(tm[:, :], pb[:, :], AF.Exp, [pb], [tm])
                self.tt("pool", self.kdec[:, b, :], self.f_tok[:, b, :], tm[:, :], ALU.mult, [self.f_tok, tm], [self.kdec])
            w = self.load_unit("w_in", l, 0)
            for h in range(4):
                pb = self.proj_fm(w, h * 128, hT)
                self.act(self.qs[:, h, :], pb[:, :], AF.Silu, [pb], [self.qs])
            pbx = self.bank()
            grp = []
            for h in range(4):
                for b in range(4):
                    c0 = (h * 4 + b) * 2
                    grp.append((pbx[:, c0:c0 + 2], self.lf[:, b, h * 128:(h + 1) * 128], self.TriX[:, :], True, True))
            self.mm(grp, [self.lf, self.TriX], [pbx])
            self.act(self.ebx[:, :], pbx[:, 0:32], AF.Exp, [pbx], [self.ebx])
            for h in range(4):
                pb = self.bank()
                self.mm([(pb[:, b * 128:(b + 1) * 128], self.lf[:, b, h * 128:(h + 1) * 128], self.TriP[:, :], True, True)
                         for b in range(4)], [self.lf, self.TriP], [pb])
                tm = self.tmpb()
                self.act(tm[:, :], pb[:, :], AF.Exp, [pb], [tm])
                self.tt("dve", self.qtT[:, h, :], self.qs[:, h, :], tm[:, :], ALU.mult, [self.qs, tm], [self.qtT])
                tm2 = self.tmpb()
                self.act(tm2[:, :], pb[:, :], AF.Exp, [pb], [tm2], scale=-1.0)
                self.tt("pool", self.ktT[:, h, :], self.omfT[:, h, :], tm2[:, :], ALU.mult, [self.omfT, tm2], [self.ktT])
            if more:
                load_x(i + 1)
            if i == 0:
                load_wbr()
            for b in range(4):
                bs = slice(b * 128, (b + 1) * 128)
                pb = self.bank()
                b0, b1 = b * 128, b * 128 + 64
                grp = []
                for h in range(4):
                    grp.append((pb[0:64, h * 128:(h + 1) * 128], self.ktT[:, h, b0:b0 + 64], self.qtT[:, h, bs], True, True))
                    grp.append((pb[64:128, h * 128 + 64:(h + 1) * 128], self.ktT[:, h, b1:b1 + 64],
                                self.qtT[:, h, b1:b1 + 64], True, True))
                self.mm(grp, [self.ktT, self.qtT], [pb])
                fill(1)
                s.op("dve", lambda e, pb=pb: e.copy_predicated(
                    self.AT[0:64, :, :], self.hmask[0:64, :, :],
                    pb[0:64, :].rearrange("p (h t) -> p h t", h=4)), [pb, self.hmask], [self.AT])
                s.op("dve", lambda e, pb=pb: e.copy_predicated(
                    self.AT[64:128, :, 64:128], self.hmask[64:128, :, 64:128],
                    pb[64:128, :].rearrange("p (h t) -> p h t", h=4)[:, :, 64:128]), [pb, self.hmask], [self.AT])
                for h in range(4):
                    c63 = (h * 4 + b) * 2
                    self.act(self.stp[:, h, :], self.St[:, h, :], AF.Identity, [self.St, self.ebx], [self.stp],
                             scale=self.ebx[:, c63:c63 + 1])
                po = self.bank()
                grp = []
                for h in range(4):
                    hs = slice(h * 128, (h + 1) * 128)
                    grp.append((po[:, hs], self.v_tok[:, b, hs], self.AT[:, h, :], True, False))
                    grp.append((po[:, hs], self.stp[:, h, :], self.qtT[:, h, bs], False, True))
                fill(1)
                self.mm(grp, [self.v_tok, self.AT, self.stp, self.qtT], [po])
                self.cp("dve", self.o_sb[:, :, bs], po[:, :].rearrange("p (h t) -> p h t", h=4), [po], [self.o_sb])
                pd = self.bank()
                self.mm([(pd[:, h * 128:(h + 1) * 128], self.kdec[:, b, h * 128:(h + 1) * 128],
                          self.v_tok[:, b, h * 128:(h + 1) * 128], True, True) for h in range(4)],
                        [self.kdec, self.v_tok], [pd])
                for h in range(4):
                    c127 = (h * 4 + b) * 2 + 1
                    self.stt("dve", self.St[:, h, :], self.St[:, h, :], self.ebx[:, c127:c127 + 1],
                             pd[:, h * 128:(h + 1) * 128], ALU.mult, ALU.add, [self.St, self.ebx, pd], [self.St])
            w = self.load_unit("w_in", l, 3)
            for h in range(4):
                pb = self.proj_fm(w, h * 128, hT)
                self.act(self.sg[:, h, :], pb[:, :], AF.Silu, [pb], [self.sg])
            sqs, pbs = [], []
            for h in range(4):
                tm = self.tmpb()
                self.act(tm[:, :], self.o_sb[:, h, :], AF.Square, [self.o_sb], [tm])
                sqs.append(tm)
            for h in range(4):
                pb = self.bank()
                self.mm([(pb[:, :], self.ones_m[:, :], sqs[h][:, :], True, True)], [self.ones_m, sqs[h]], [pb])
                pbs.append(pb)
            for h in range(4):
                tm2 = self.tmpb()
                self.rsqrt_el(tm2[:, :], pbs[h][:, :], [pbs[h]], [tm2])
                tm3 = self.tmpb()
                self.stt("dve", tm3[:, :], self.o_sb[:, h, :], self.hnw[:, 0:1], tm2[:, :], ALU.mult, ALU.mult,
                         [self.o_sb, self.hnw, tm2], [tm3])
                self.tt("pool", self.yh[:, h, :], tm3[:, :], self.sg[:, h, :], ALU.mult, [tm3, self.sg], [self.yh])
            if more:
                self.norm_part1(self.xt, sms[(i + 1) % 2])
            w = self.load_unit("w_in", l, 7)
            for g in range(4):
                pb = self.proj_fm(w, g * 128, hT)
                self.act(self.ug[:, g, :], pb[:, :], AF.Gelu, [pb], [self.ug])
            w = self.load_unit("w_in", l, 8)
            for b in range(4):
                pb = self.proj_tm(w, b, hT)
                tm = self.tmpb()
                self.act(tm[:, :], pb[:, :], AF.Gelu, [pb], [tm])
                bn = self.bnst
                s.op("dve", lambda e, tm=tm, bn=bn: e.bn_stats(bn[:, 0:6], tm[:, :]), [tm], [bn])
                s.op("dve", lambda e, bn=bn: e.bn_aggr(bn[:, 6:8], bn[:, 0:6]), [bn], [bn])
                self.rsqrt(bn[:, 7:8], bn[:, 7:8], [bn], [bn])
                self.ts("dve", self.vhat[:, b, :], tm[:, :], bn[:, 6:7], bn[:, 7:8], ALU.subtract, ALU.mult,
                        [tm, bn], [self.vhat])
            fill(4)
            if more:
                self.transpose_to(self.hb, hTs[(i + 1) % 2])
            for g in range(4):
                gs = slice(g * 128, (g + 1) * 128)
                pb = self.bank()
                self.mm([(pb[:, b * 128:(b + 1) * 128], self.vhat[:, b, gs], self.WsT[:, g, :], True, True)
                         for b in range(4)], [self.vhat, self.WsT], [pb])
                tm = self.tmpb()
                for b in range(4):
                    self.stt("dve", tm[:, b * 128:(b + 1) * 128], pb[:, b * 128:(b + 1) * 128],
                             self.lnw_col[:, g:g + 1], self.Cg[:, g, :], ALU.mult, ALU.add,
                             [pb, self.lnw_col, self.Cg], [tm])
                self.tt("dve", self.yg[:, g, :], tm[:, :], self.ug[:, g, :], ALU.mult, [tm, self.ug], [self.yg])
            fill(100)
            for half in range(2):
                wh = self.load_unit("w_in", l, 9 + half)
                for q in range(4):
                    dc = half * 4 + q
                    pb = self.proj_fm(wh, q * 128, hT)
                    gh = self.tmpb()
                    self.act(gh[:, :], pb[:, :], AF.Sigmoid, [pb], [gh])
                    pb2 = self.bank()
                    self.mm([(pb2[:, :], self.wbr_h[:, mc, dc * 128:(dc + 1) * 128], self.yh[:, mc, :], mc == 0, mc == 3)
                             for mc in range(4)], [self.wbr_h, self.yh], [pb2])
                    acc = (self.qs if dc < 4 else self.omfT)
                    self.tt("dve", acc[:, dc % 4, :], pb2[:, :], gh[:, :], ALU.mult, [pb2, gh], [acc])
            for half in range(2):
                wh = self.load_unit("w_in", l, 13 + half)
                for q in range(4):
                    dc = half * 4 + q
                    pb = self.proj_fm(wh, q * 128, hT)
                    gg = self.tmpb()
                    self.act(gg[:, :], pb[:, :], AF.Sigmoid, [pb], [gg])
                    pb2 = self.bank()
                    self.mm([(pb2[:, :], self.wbr_g[:, mc, dc * 128:(dc + 1) * 128], self.yg[:, mc, :], mc == 0, mc == 3)
                             for mc in range(4)], [self.wbr_g, self.yg], [pb2])
                    tm = self.tmpb()
                    self.tt("dve", tm[:, :], pb2[:, :], gg[:, :], ALU.mult, [pb2, gg], [tm])
                    acc = (self.qs if dc < 4 else self.omfT)
                    st = self.stp32[stp_i % 2]
                    stp_i += 1
                    self.tt("pool", st[:, :], acc[:, dc % 4, :], tm[:, :], ALU.add, [acc, tm], [st])
                    s.dma("pool", self.pm.ap()[dc, :, tsl], st[:, :], [st], [self.pm_b[i]], st)
            stq_i = fst["q"]
        self.run_conv_jobs(10 ** 6, max_layer=l)

    def phase_b1(self, l):
        s, sb = self.s, self.sb
        S, NT = self.S, self.NT
        KTh = [sb(f"KTh{j}", (128, S), BF16) for j in range(2)]
        Vh = [sb(f"Vh{j}", (128, S // 128, 128), BF16) for j in range(2)]
        qts = [sb(f"qt{j}", (128, T), BF16) for j in range(3)]
        PT = [sb(f"PT{j}", (128, T), BF16) for j in range(12)]
        sty = [sb(f"sty{j}", (128, T), BF16) for j in range(2)]
        ev = [[sb(f"ev{j}_{k}", (128, T), F32) for k in range(3)] for j in range(2)]
        acc = self.banks[0:3]
        rot = self.banks[3:8]
        NR = len(rot)
        st = {"rot": 0, "pt": 0, "sty": 0}

        def load_head(h):
            s.dma("sp", KTh[h % 2][:, :], self.kT.ap()[h], self.kT_b, [KTh[h % 2]], KTh[h % 2])
            s.dma("sp", Vh[h % 2][:, :, :], self.vA.ap()[h], self.vA_b, [Vh[h % 2]], Vh[h % 2])

        def s_emit(h, i, qt, jb):
            r = jb - 4 * i
            q0 = max(r, 0) * 128
            ks = slice(jb * 128, (jb + 1) * 128)
            pts = []
            for c in range(2):
                ps = rot[st["rot"] % NR]
                st["rot"] += 1
                cs = slice(c * 64, (c + 1) * 64)
                self.mm([(ps[:, q0:T], KTh[h % 2][cs, ks], qt[cs, q0:T], True, True)], [KTh[h % 2], qt], [ps])
                pt = PT[st["pt"] % 12]
                st["pt"] += 1
                if r >= 1:
                    s.op("pool", lambda e, pt=pt, q0=q0: e.memset(pt[:, 0:q0], 0.0), [], [pt])
                self.act(pt[:, q0:T], ps[:, q0:T], AF.Exp, [ps], [pt], scale=0.125)
                if r >= 0:
                    self.tt("dve", pt[:, q0:q0 + 128], pt[:, q0:q0 + 128], self.amask[:, :], ALU.mult,
                            [pt, self.amask], [pt])
                pts.append(pt)
            return pts

        def pv_emit(h, jb, nkb, pts):
            first, last = (jb == 0), (jb == nkb - 1)
            grp = []
            for c in range(2):
                grp.append((acc[c][:, :], Vh[h % 2][:, jb, :], pts[c][:, :], first, last))
            rd = [Vh[h % 2], self.ones_b] + pts
            if jb % 2 == 0:
                st["prev_pts"] = pts
            else:
                pp = st["prev_pts"]
                for t_, ptile in enumerate((pp[0], pp[1], pts[0], pts[1])):
                    grp.append((acc[2][32 * t_:32 * t_ + 32, :], self.ones_b[:, 0:32], ptile[:, :],
                                jb == 1, last, (0, 32 * t_)))
                rd = rd + list(pp)
            self.mm(grp, rd, acc)

        def finalize(h, i, e):
            self.cp("dve", e[0][:, :], acc[0][:, :], [acc[0]], [e[0]])
            self.cp("dve", e[1][:, :], acc[1][:, :], [acc[1]], [e[1]])
            self.cp("dve", e[2][:, :], acc[2][:, :], [acc[2]], [e[2]])

        def finalize2a(h, i, e):
            pa = rot[st["rot"] % NR]
            st["rot"] += 1
            self.mm([(pa[:, :], self.sel1[:, :], e[2][:, :], True, True)], [self.sel1, e[2]], [pa])
            pb2 = rot[st["rot"] % NR]
            st["rot"] += 1
            self.mm([(pb2[:, :], self.sel2[:, :], e[2][:, :], True, True)], [self.sel2, e[2]], [pb2])
            s.op("dve", lambda en: en.reciprocal(e[2][:, :], pa[:, :]), [pa], [e[2]])
            self.tt("dve", e[0][:, :], e[0][:, :], e[2][:, :], ALU.mult, [e[0], e[2]], [e[0]])
            s.op("dve", lambda en: en.reciprocal(e[2][:, :], pb2[:, :]), [pb2], [e[2]])
            self.tt("dve", e[1][:, :], e[1][:, :], e[2][:, :], ALU.mult, [e[1], e[2]], [e[1]])
            self.stt("dve", e[0][:, :], e[1][:, :], self.lam[:, 1:2], e[0][:, :], ALU.mult, ALU.add,
                     [e[0], e[1], self.lam], [e[0]])
            self.tt("dve", e[1][:, :], e[0][:, :], e[0][:, :], ALU.mult, [e[0]], [e[1]])

        def finalize2(h, i, e):
            tsl = slice(i * T, (i + 1) * T)
            pb = rot[st["rot"] % NR]
            st["rot"] += 1
            self.mm([(pb[:, :], self.ones_m[:, :], e[1][:, :], True, True)], [self.ones_m, e[1]], [pb])
            self.rsqrt_el(e[2][:, :], pb[:, :], [pb], [e[2]])
            so = sty[st["sty"] % 2]
            st["sty"] += 1
            self.stt("dve", so[:, :], e[0][:, :], self.dnw[:, 0:1], e[2][:, :], ALU.mult, ALU.mult,
                     [e[0], self.dnw, e[2]], [so])
            s.dma("pool", self.yA.ap()[h, :, tsl], so[:, :], [so], [self.yA_b[i]], so)

        from collections import deque
        yat = sb("yat", (128, 4, T), BF16)
        pmt = sb("pmt", (128, 8, T), F32)
        gat = sb("gat", (128, 8, T), BF16)
        mgT = sb("mgT", (128, 8, T), BF16)
        b2q = deque()

        def rb():
            b_ = rot[st["rot"] % NR]
            st["rot"] += 1
            return b_

        def b2_make(i):
            tsl = slice(i * T, (i + 1) * T)
            steps = []

            def s_load():
                s.dma("sp", yat[:, :, :], self.yA.ap()[:, :, tsl].rearrange("h p t -> p h t"), [self.yA_b[i]], [yat], yat)
                s.dma("sp", pmt[:, :, :], self.pm.ap()[:, :, tsl].rearrange("c p t -> p c t"), [self.pm_b[i]], [pmt], pmt)
                s.dma("sp", gat[:, :, :], self.gA.ap()[:, :, tsl].rearrange("c p t -> p c t"), [self.gA_b[i]], [gat], gat)
            steps.append(s_load)
            steps.extend([None] * 20)
            wst = {}

            def mk_br(dc):
                def f():
                    if dc == 0:
                        wst["w"] = self.load_unit("w_br_attn", l, 0)
                    w = wst["w"]
                    wv = w[:, :].rearrange("p (c j) -> p c j", c=4)
                    pb = rb()
                    self.mm([(pb[:, :], wv[:, mc, dc * 128:(dc + 1) * 128], yat[:, mc, :], mc == 0, mc == 3)
                             for mc in range(4)], [w, yat], [pb])
                    tm = self.tmpb()
                    self.tt("dve", tm[:, :], pb[:, :], gat[:, dc, :], ALU.mult, [pb, gat], [tm])
                    self.tt("pool", mgT[:, dc, :], tm[:, :], pmt[:, dc, :], ALU.add, [tm, pmt], [mgT])
                    if dc == 7:
                        s.dma("pool", self.mg.ap()[:, :, tsl].rearrange("c p t -> p c t"), mgT[:, :, :], [mgT],
                              [self.mg_b[i]], mgT)
                return f

            for dc in range(8):
                steps.append(mk_br(dc))
            return steps

        tiles = [(h, i) for h in range(4) for i in range(NT)]
        load_head(0)

        blocks = []
        for n, (h, i) in enumerate(tiles):
            for jb in range(4 * (i + 1)):
                blocks.append((n, h, i, jb))
        qt_of, pts_of = {}, {}
        sp = [0]
        LOOK = 2

        def emit_s_until(target):
            while sp[0] < min(target, len(blocks)):
                n, h, i, jb = blocks[sp[0]]
                if jb == 0:
                    qt = qts[n % 3]
                    s.dma("sp", qt[:, :], self.qT.ap()[h, :, i * T:(i + 1) * T], [self.qT_b[i]], [qt], qt)
                    qt_of[n] = qt
                pts_of[sp[0]] = s_emit(h, i, qt_of[n], jb)
                sp[0] += 1

        fin2 = None
        for k, (n, h, i, jb) in enumerate(blocks):
            nkb = 4 * (i + 1)
            if jb == 0:
                self.run_conv_jobs(1)
            if jb == 0 and i == 0 and h + 1 < 4:
                load_head(h + 1)
            if jb == 2 and fin2 is not None:
                finalize2a(*fin2)
            if jb == min(7, nkb - 1) and fin2 is not None:
                finalize2(*fin2)
                if fin2[0] == 3:
                    b2q.extend(b2_make(fin2[1]))
                fin2 = None
            emit_s_until(k + 1 + LOOK)
            pv_emit(h, jb, nkb, pts_of.pop(k))
            if b2q:
                f_ = b2q.popleft()
                if f_ is not None:
                    f_()
            if jb == nkb - 1:
                finalize(h, i, ev[n % 2])
                fin2 = (h, i, ev[n % 2])
        finalize2a(*fin2)
        finalize2(*fin2)
        b2q.extend(b2_make(fin2[1]))
        while b2q:
            f_ = b2q.popleft()
            if f_ is not None:
                f_()
        self.run_conv_jobs(10 ** 6)

    def phase_c(self, l):
        s, sb = self.s, self.sb
        if True:
            self.aT = sb("aT", (128, 32, T), BF16)
            self.wff2 = sb("wff2", (128, 32, D), BF16)
        self.bcast_row(self.nw_rep, self.i["norm_ff_w"].ap()[l:l + 1, :], D)
        def load_wff2():
            for u in range(8):
                s.dma("sp", self.wff2[:, u * 4:(u + 1) * 4, :],
                      self.wb[("w_ff2", l)].ap()[u].rearrange("p (c j) -> p c j", c=4),
                      [self.wb_b[("w_ff2", l)]], [self.wff2], self.wff2)
        last = (l == self.L - 1)
        if last:
            fw_rep = sb("fw_rep", (128, D), F32)
            self.bcast_row(fw_rep, self.i["final_norm_w"].ap()[0:1, :], D)
            sm3 = sb("small3", (128, 64), F32)
        xt2 = sb("xt2", (128, 4, D), F32)
        hT2 = sb("hT2c", (128, 8, T), BF16)
        xts = [self.xt, xt2]
        hTs = [self.hT, hT2]
        sms = [self.small, self.small2]
        NT = self.NT

        xsrc = self.x_in.ap() if l == 0 else self.xs.ap()
        xb = self.x_b if l == 0 else self.xs_b
        mgt = sb("mgt", (128, 8, T), BF16)

        def load_x(i):
            xt = xts[i % 2]
            s.dma("sp", xt[:, :, :], xsrc[i * T:(i + 1) * T, :].rearrange("(b p) d -> p b d", p=128),
                  [xb[i]], [xt], xt)
            s.dma("sp", mgt[:, :, :], self.mg.ap()[:, :, i * T:(i + 1) * T].rearrange("c p t -> p c t"),
                  [self.mg_b[i]], [mgt], mgt)

        def outproj(i):
            xt = xts[i % 2]
            for u in range(2):
                w = self.load_unit("w_out", l, u)
                wv = w[:, :].rearrange("p (c j) -> p c j", c=8)
                for b in range(4):
                    pb = self.bank()
                    self.mm([(pb[:, :], mgt[:, dc, b * 128:(b + 1) * 128], wv[:, dc, :], dc == 0, dc == 7)
                             for dc in range(8)], [w, mgt], [pb])
                    self.tt("dve", xt[:, b, u * 512:(u + 1) * 512], xt[:, b, u * 512:(u + 1) * 512], pb[:, :],
                            ALU.add, [xt, pb], [xt])

        load_x(0)
        outproj(0)
        self.norm_part1(xts[0], sms[0])
        self.transpose_to(self.hb, hTs[0])
        for i in range(NT):
            tsl = slice(i * T, (i + 1) * T)
            xt, hT = xts[i % 2], hTs[i % 2]
            more = (i + 1 < NT)
            for u in range(8):
                w = self.load_unit("w_ff1", l, u)
                if i == 0 and u == 0:
                    load_wff2()
                if u == 3 and more:
                    load_x(i + 1)
                for q in range(4):
                    fc = u * 4 + q
                    pb = self.proj_fm(w, q * 128, hT)
                    tm = self.tmpb()
                    self.act(tm[:, :], pb[:, :], AF.Relu, [pb], [tm])
                    self.tt("dve" if fc % 2 == 0 else "pool", self.aT[:, fc, :], tm[:, :], tm[:, :], ALU.mult, [tm], [self.aT])
            if more:
                outproj(i + 1)
                self.norm_part1(xts[(i + 1) % 2], sms[(i + 1) % 2])
            for b in range(4):
                if b == 2 and more:
                    self.transpose_to(self.hb, hTs[(i + 1) % 2])
                for u in range(2):
                    pb = self.bank()
                    self.mm([(pb[:, :], self.aT[:, fc, b * 128:(b + 1) * 128], self.wff2[:, fc, u * 512:(u + 1) * 512],
                              fc == 0, fc == 31) for fc in range(32)], [self.aT, self.wff2], [pb])
                    self.tt("dve", xt[:, b, u * 512:(u + 1) * 512], xt[:, b, u * 512:(u + 1) * 512], pb[:, :],
                            ALU.add, [xt, pb], [xt])
            if last:
                self.rms_stats(xt, sm3)
                for b in range(4):
                    self.stt("dve", xt[:, b, :], xt[:, b, :], sm3[:, 28 + b:29 + b], fw_rep[:, :], ALU.mult, ALU.mult,
                             [xt, sm3, fw_rep], [xt])
                s.dma("pool", self.y_out.ap()[tsl, :].rearrange("(b p) d -> p b d", p=128), xt[:, :, :], [xt],
                      [self.y_b[i]], xt)
            else:
                s.dma("pool", self.xs.ap()[tsl, :].rearrange("(b p) d -> p b d", p=128), xt[:, :, :], [xt],
                      [self.xs_b[i]], xt)


_NAMES = ["norm_mix_w", "w_in", "hgrn_lb_logits", "hgrn_norm_w", "diff_lam_q1", "diff_lam_k1", "diff_lam_q2",
          "diff_lam_k2", "diff_norm_w", "gmlp_ln_w", "gmlp_ln_b", "gmlp_w_s", "gmlp_b_s", "w_br_hgrn", "w_br_attn",
          "w_br_gmlp", "w_out", "norm_ff_w", "w_ff1", "w_ff2"]


def run(inputs, S, n_cores):
    b = Builder(S)
    nc = b.build()
    x = np.ascontiguousarray(np.asarray(inputs["x"], dtype=np.float32))
    common = {n: np.ascontiguousarray(np.asarray(inputs[n], dtype=np.float32)) for n in _NAMES}
    common["final_norm_w"] = np.ascontiguousarray(np.asarray(inputs["final_norm_w"], dtype=np.float32).reshape(1, D))
    in_maps = []
    for c in range(n_cores):
        m = dict(common)
        m["x"] = np.ascontiguousarray(x[c])
        in_maps.append(m)
    res = run_bass_kernel_spmd(nc, in_maps, core_ids=list(range(n_cores)))
    return np.stack([np.asarray(r["y"]) for r in res.results], axis=0).astype(np.float32)


def kernel(**inputs):
    x = np.asarray(inputs["x"])
    return run(inputs, x.shape[1], x.shape[0])
```
